# Optimizing a Trainium2 kernel written in Bass

```python
import math
import jax, jax.numpy as jnp
from jax import lax
import numpy as np

D_MODEL = 2048
BATCH = 2
SEQ = 8192
DEPTH = 2

D_FF = 5632
NORM_EPS = 1e-6

W_LRU = 1024
LRU_HEADS = 8
LRU_BLOCK = W_LRU // LRU_HEADS
CONV_W = 4
CONV_PAD_L = 2
LRU_C = 8.0

W_MLSTM = 1024
MLSTM_HEADS = 4
MLSTM_DH = W_MLSTM // MLSTM_HEADS
MLSTM_CHUNK = 64

AB_SPLITS = (W_LRU, W_LRU, W_MLSTM, W_MLSTM, W_MLSTM, W_MLSTM, 4 * MLSTM_HEADS)
AB_IN = 2 * W_LRU + 4 * W_MLSTM + 4 * MLSTM_HEADS

GLA_HEADS = 4
GLA_DK = 128
GLA_DV = 256
GLA_QK = GLA_HEADS * GLA_DK
GLA_V = GLA_HEADS * GLA_DV
GLA_RANK = 16
GLA_TAU = 16.0
GLA_CHUNK = 64

S5_W = 1024
S5_GROUP = 16
S5_GROUPS = S5_W // S5_GROUP
S5_P = 64
S5_DT_MIN = 0.001
S5_DT_MAX = 0.1

CD_SPLITS = (GLA_QK, GLA_QK, GLA_V, GLA_V, 2 * GLA_RANK, S5_W)
CD_IN = 2 * GLA_QK + 2 * GLA_V + 2 * GLA_RANK + S5_W

MIX_OUT = W_LRU + W_MLSTM

kernel_name = 'hybrid_bidir_rglru_mlstm_gla_s5_macaron'


def _split(t, sizes):
    bounds = [int(b) for b in np.cumsum(sizes)[:-1]]
    return jnp.split(t, bounds, axis=-1)


def rms_norm(x, g):
    xf = x.astype(jnp.float32)
    y = xf * lax.rsqrt(jnp.mean(xf * xf, axis=-1, keepdims=True) + NORM_EPS)
    return (y * g.astype(jnp.float32)).astype(x.dtype)


def headwise_rms_norm(t, g):
    y = t * lax.rsqrt(jnp.mean(t * t, axis=-1, keepdims=True) + NORM_EPS)
    return y.reshape(t.shape[:2] + (-1,)) * g


def swiglu(h, w_gu, w_down):
    g, u = jnp.split(h @ w_gu, 2, axis=-1)
    return (jax.nn.silu(g) * u) @ w_down


def _heads(t, n):
    return t.reshape(t.shape[:2] + (n, -1)).transpose(0, 2, 1, 3)


def _flip_seq(t):
    return jnp.flip(t, axis=2)


def _to_chunks(t, chunk):
    b, h, s = t.shape[:3]
    t = t.reshape((b, h, s // chunk, chunk) + t.shape[3:])
    return jnp.moveaxis(t, 2, 0)


def _from_chunks(t):
    nc, b, h, l = t.shape[:4]
    return jnp.moveaxis(t, 0, 2).reshape((b, h, nc * l) + t.shape[4:])


def _affine_combine(e1, e2):
    a1, b1 = e1
    a2, b2 = e2
    return a1 * a2, a2 * b1 + b2


def _complex_affine_combine(e1, e2):
    a1r, a1i, b1r, b1i = e1
    a2r, a2i, b2r, b2i = e2
    ar = a1r * a2r - a1i * a2i
    ai = a1r * a2i + a1i * a2r
    br = a2r * b1r - a2i * b1i + b2r
    bi = a2r * b1i + a2i * b1r + b2i
    return ar, ai, br, bi


def centred_depthwise_conv(x, w, b):
    c = x.shape[-1]
    y = lax.conv_general_dilated(
        x, w[:, None, :].astype(x.dtype), window_strides=(1,),
        padding=[(CONV_PAD_L, CONV_W - 1 - CONV_PAD_L)],
        dimension_numbers=('NWC', 'WIO', 'NWC'), feature_group_count=c)
    return y + b


def block_diag_linear(x, w, b):
    xh = x.reshape(x.shape[:2] + (LRU_HEADS, LRU_BLOCK))
    return jnp.einsum('bshi,hij->bshj', xh, w).reshape(x.shape) + b


def rg_lru_direction(x, w_r, b_r, w_i, b_i, lam, reverse):
    r = jax.nn.sigmoid(block_diag_linear(x, w_r, b_r))
    i = jax.nn.sigmoid(block_diag_linear(x, w_i, b_i))
    log_a = -LRU_C * r * jax.nn.softplus(-lam)
    a = jnp.exp(log_a)
    b = jnp.sqrt(-jnp.expm1(2.0 * log_a)) * (i * x)
    _, h = lax.associative_scan(_affine_combine, (a, b), axis=1, reverse=reverse)
    return h


def mlstm_chunkwise(q, k, v, ig, lf):
    bsz, nh, _, dh = q.shape
    L = MLSTM_CHUNK
    mask = jnp.tril(jnp.ones((L, L), dtype=bool))
    xs = tuple(_to_chunks(t, L) for t in (q, k, v, ig, lf))

    def step(carry, inp):
        c_st, n_st, m_st = carry
        qt, kt, vt, it, ft = inp
        cum = jnp.cumsum(ft, axis=-1)
        dmat = jnp.where(mask, cum[..., :, None] - cum[..., None, :] + it[..., None, :], -jnp.inf)
        m_inter = cum + m_st[..., None]
        m_t = jnp.maximum(jnp.max(dmat, axis=-1), m_inter)
        w_intra = jnp.exp(dmat - m_t[..., None])
        w_inter = jnp.exp(m_inter - m_t)
        s = jnp.einsum('bhtd,bhsd->bhts', qt, kt) * w_intra
        num = jnp.einsum('bhts,bhse->bhte', s, vt) + w_inter[..., None] * jnp.einsum('bhtd,bhde->bhte', qt, c_st)
        den = jnp.sum(s, axis=-1) + w_inter * jnp.einsum('bhtd,bhd->bht', qt, n_st)
        h = num / jnp.maximum(jnp.abs(den), jnp.exp(-m_t))[..., None]
        tot = cum[..., -1]
        dec_s = tot[..., None] - cum + it
        m_new = jnp.maximum(tot + m_st, jnp.max(dec_s, axis=-1))
        ws = jnp.exp(dec_s - m_new[..., None])
        wc = jnp.exp(tot + m_st - m_new)
        c_new = wc[..., None, None] * c_st + jnp.einsum('bhs,bhsd,bhse->bhde', ws, kt, vt)
        n_new = wc[..., None] * n_st + jnp.einsum('bhs,bhsd->bhd', ws, kt)
        return (c_new, n_new, m_new), h

    init = (jnp.zeros((bsz, nh, dh, dh), q.dtype), jnp.zeros((bsz, nh, dh), q.dtype),
            jnp.zeros((bsz, nh), q.dtype))
    _, hs = lax.scan(step, init, xs)
    return _from_chunks(hs)


def mixer_ab(h, w_in, conv_w, conv_b, lru_gate_w, lru_gate_b, lru_lambda, mlstm_gate_b, mlstm_norm, w_out):
    bsz, s, _ = h.shape
    proj = (h @ w_in).astype(jnp.float32)
    xr, gr, q, k, v, o, gates = _split(proj, AB_SPLITS)
    xr = centred_depthwise_conv(xr, conv_w, conv_b)
    h_lru = (rg_lru_direction(xr, lru_gate_w[0, 0], lru_gate_b[0, 0], lru_gate_w[0, 1], lru_gate_b[0, 1],
                              lru_lambda[0], False)
             + rg_lru_direction(xr, lru_gate_w[1, 0], lru_gate_b[1, 0], lru_gate_w[1, 1], lru_gate_b[1, 1],
                                lru_lambda[1], True))
    y_a = jax.nn.gelu(gr) * h_lru
    q = _heads(q, MLSTM_HEADS) * (MLSTM_DH ** -0.5)
    k = _heads(k, MLSTM_HEADS)
    v = _heads(v, MLSTM_HEADS)
    g = (gates.reshape(bsz, s, 2, 2, MLSTM_HEADS) + mlstm_gate_b).transpose(2, 3, 0, 4, 1)
    h_f = mlstm_chunkwise(q, k, v, g[0, 0], jax.nn.log_sigmoid(g[0, 1]))
    h_b = _flip_seq(mlstm_chunkwise(_flip_seq(q), _flip_seq(k), _flip_seq(v), _flip_seq(g[1, 0]),
                                    _flip_seq(jax.nn.log_sigmoid(g[1, 1]))))
    hm = headwise_rms_norm((h_f + h_b).transpose(0, 2, 1, 3), mlstm_norm)
    y_b = jax.nn.sigmoid(o) * hm
    y = jnp.concatenate([y_a, y_b], axis=-1)
    return y.astype(h.dtype) @ w_out


def gla_chunked(q, k, v, la):
    bsz, nh, _, dk = q.shape
    dv = v.shape[-1]
    L = GLA_CHUNK
    mask = jnp.tril(jnp.ones((L, L), dtype=bool))[..., None]
    xs = tuple(_to_chunks(t, L) for t in (q, k, v, la))

    def step(s_st, inp):
        qt, kt, vt, at = inp
        cum = jnp.cumsum(at, axis=2)
        rel = jnp.where(mask, cum[:, :, :, None, :] - cum[:, :, None, :, :], -jnp.inf)
        attn = jnp.einsum('bhtd,bhsd,bhtsd->bhts', qt, kt, jnp.exp(rel))
        o = jnp.einsum('bhts,bhse->bhte', attn, vt) + jnp.einsum('bhtd,bhde->bhte', qt * jnp.exp(cum), s_st)
        tot = cum[:, :, -1]
        s_new = jnp.exp(tot)[..., None] * s_st + jnp.einsum('bhsd,bhse->bhde', kt * jnp.exp(tot[:, :, None] - cum), vt)
        return s_new, o

    _, os_ = lax.scan(step, jnp.zeros((bsz, nh, dk, dv), q.dtype), xs)
    return _from_chunks(os_)


def s5_discretise(a_re, a_im, log_dt, b_re, b_im):
    dt = jnp.exp(log_dt)[:, None]
    mag = jnp.exp(dt * a_re)
    lr = mag * jnp.cos(dt * a_im)
    li = mag * jnp.sin(dt * a_im)
    den = a_re * a_re + a_im * a_im
    nr = lr - 1.0
    cr = (nr * a_re + li * a_im) / den
    ci = (li * a_re - nr * a_im) / den
    bbr = cr[..., None] * b_re - ci[..., None] * b_im
    bbi = cr[..., None] * b_im + ci[..., None] * b_re
    return lr, li, bbr, bbi


def s5_scan(ug, lr, li, bbr, bbi, reverse):
    bu_r = jnp.einsum('bsgc,gpc->bsgp', ug, bbr)
    bu_i = jnp.einsum('bsgc,gpc->bsgp', ug, bbi)
    ar = jnp.broadcast_to(lr, bu_r.shape)
    ai = jnp.broadcast_to(li, bu_r.shape)
    _, _, xr, xi = lax.associative_scan(_complex_affine_combine, (ar, ai, bu_r, bu_i), axis=1, reverse=reverse)
    return xr, xi


def s5_mixer(u, a_re, a_im, log_dt, b_re, b_im, c_re, c_im, d, w_glu):
    bsz, s, _ = u.shape
    ug = u.reshape(bsz, s, S5_GROUPS, S5_GROUP)
    fr, fi = s5_scan(ug, *s5_discretise(a_re[0], a_im[0], log_dt[0], b_re, b_im), False)
    br_, bi_ = s5_scan(ug, *s5_discretise(a_re[1], a_im[1], log_dt[1], b_re, b_im), True)
    xr = fr + br_
    xi = fi + bi_
    y = (jnp.einsum('bsgp,gcp->bsgc', xr, c_re) - jnp.einsum('bsgp,gcp->bsgc', xi, c_im)
         + d.reshape(S5_GROUPS, S5_GROUP) * ug)
    y = jax.nn.gelu(y.reshape(bsz, s, S5_W))
    return y * jax.nn.sigmoid(y @ w_glu)


def mixer_cd(h, w_in, gla_w_gate2, gla_gate_b, gla_norm, s5_a_re, s5_a_im, s5_log_dt,
             s5_b_re, s5_b_im, s5_c_re, s5_c_im, s5_d, s5_w_glu, w_out):
    bsz, s, _ = h.shape
    proj = (h @ w_in).astype(jnp.float32)
    q, k, v, r, glr, u = _split(proj, CD_SPLITS)
    q = _heads(q, GLA_HEADS) * (GLA_DK ** -0.5)
    k = _heads(k, GLA_HEADS)
    v = _heads(v, GLA_HEADS)
    low = glr.reshape(bsz, s, 2, GLA_RANK)
    gate_pre = jnp.einsum('bsdr,drk->dbsk', low, gla_w_gate2) + gla_gate_b[:, None, None, :]
    la = (jax.nn.log_sigmoid(gate_pre) / GLA_TAU).reshape(2, bsz, s, GLA_HEADS, GLA_DK).transpose(0, 1, 3, 2, 4)
    o_f = gla_chunked(q, k, v, la[0])
    o_b = _flip_seq(gla_chunked(_flip_seq(q), _flip_seq(k), _flip_seq(v), _flip_seq(la[1])))
    y_c = headwise_rms_norm((o_f + o_b).transpose(0, 2, 1, 3), gla_norm) * jax.nn.silu(r)
    y_d = s5_mixer(u, s5_a_re, s5_a_im, s5_log_dt, s5_b_re, s5_b_im, s5_c_re, s5_c_im, s5_d, s5_w_glu)
    y = jnp.concatenate([y_c, y_d], axis=-1)
    return y.astype(h.dtype) @ w_out


def setup_inputs(seed: int = 0) -> dict:
    key = jax.random.key(seed)
    keys = iter(jax.random.split(key, 48))
    n_even = (DEPTH + 1) // 2
    n_odd = DEPTH // 2
    f32 = jnp.float32

    def normal(shape, scale):
        return scale * jax.random.normal(next(keys), shape, f32)

    def gain(shape):
        return 1.0 + 0.02 * jax.random.normal(next(keys), shape, f32)

    def uniform(shape, lo, hi):
        return jax.random.uniform(next(keys), shape, f32, lo, hi)

    x = normal((BATCH, SEQ, D_MODEL), 1.0)
    norm_ffn1 = gain((DEPTH, D_MODEL))
    ffn1_w_gu = normal((DEPTH, D_MODEL, 2 * D_FF), D_MODEL ** -0.5)
    ffn1_w_down = normal((DEPTH, D_FF, D_MODEL), D_FF ** -0.5)
    norm_mix = gain((DEPTH, D_MODEL))
    norm_ffn2 = gain((DEPTH, D_MODEL))
    ffn2_w_gu = normal((DEPTH, D_MODEL, 2 * D_FF), D_MODEL ** -0.5)
    ffn2_w_down = normal((DEPTH, D_FF, D_MODEL), D_FF ** -0.5)

    ab_w_in = normal((n_even, D_MODEL, AB_IN), D_MODEL ** -0.5)
    lru_conv_w = normal((n_even, CONV_W, W_LRU), CONV_W ** -0.5)
    lru_conv_b = normal((n_even, W_LRU), 0.02)
    lru_gate_w = normal((n_even, 2, 2, LRU_HEADS, LRU_BLOCK, LRU_BLOCK), LRU_BLOCK ** -0.5)
    lru_gate_b = normal((n_even, 2, 2, W_LRU), 0.02)
    a_c = uniform((n_even, 2, W_LRU), 0.9, 0.999)
    a0 = a_c ** (1.0 / LRU_C)
    lru_lambda = jnp.log(a0) - jnp.log1p(-a0)
    i_bias = normal((n_even, 2, MLSTM_HEADS), 0.1)
    f_bias = jnp.linspace(3.0, 6.0, MLSTM_HEADS, dtype=f32) + normal((n_even, 2, MLSTM_HEADS), 0.1)
    mlstm_gate_b = jnp.stack([i_bias, f_bias], axis=2)
    mlstm_norm = gain((n_even, W_MLSTM))
    ab_w_out = normal((n_even, MIX_OUT, D_MODEL), MIX_OUT ** -0.5)

    cd_w_in = normal((n_odd, D_MODEL, CD_IN), D_MODEL ** -0.5)
    gla_w_gate2 = normal((n_odd, 2, GLA_RANK, GLA_QK), GLA_RANK ** -0.5)
    gla_gate_b = normal((n_odd, 2, GLA_QK), 0.1)
    gla_norm = gain((n_odd, GLA_V))
    s5_a_re = -0.5 + normal((n_odd, 2, S5_GROUPS, S5_P), 0.01)
    s5_a_im = math.pi * jnp.arange(S5_P, dtype=f32) + normal((n_odd, 2, S5_GROUPS, S5_P), 0.01)
    s5_log_dt = uniform((n_odd, 2, S5_GROUPS), math.log(S5_DT_MIN), math.log(S5_DT_MAX))
    s5_b_re = normal((n_odd, S5_GROUPS, S5_P, S5_GROUP), (2 * S5_GROUP) ** -0.5)
    s5_b_im = normal((n_odd, S5_GROUPS, S5_P, S5_GROUP), (2 * S5_GROUP) ** -0.5)
    s5_c_re = normal((n_odd, S5_GROUPS, S5_GROUP, S5_P), S5_P ** -0.5)
    s5_c_im = normal((n_odd, S5_GROUPS, S5_GROUP, S5_P), S5_P ** -0.5)
    s5_d = normal((n_odd, S5_W), 1.0)
    s5_w_glu = normal((n_odd, S5_W, S5_W), S5_W ** -0.5)
    cd_w_out = normal((n_odd, MIX_OUT, D_MODEL), MIX_OUT ** -0.5)
    final_norm = gain((D_MODEL,))
    return {
        'x': x,
        'norm_ffn1': norm_ffn1, 'ffn1_w_gu': ffn1_w_gu, 'ffn1_w_down': ffn1_w_down,
        'norm_mix': norm_mix,
        'norm_ffn2': norm_ffn2, 'ffn2_w_gu': ffn2_w_gu, 'ffn2_w_down': ffn2_w_down,
        'ab_w_in': ab_w_in, 'lru_conv_w': lru_conv_w, 'lru_conv_b': lru_conv_b,
        'lru_gate_w': lru_gate_w, 'lru_gate_b': lru_gate_b, 'lru_lambda': lru_lambda,
        'mlstm_gate_b': mlstm_gate_b, 'mlstm_norm': mlstm_norm, 'ab_w_out': ab_w_out,
        'cd_w_in': cd_w_in, 'gla_w_gate2': gla_w_gate2, 'gla_gate_b': gla_gate_b, 'gla_norm': gla_norm,
        's5_a_re': s5_a_re, 's5_a_im': s5_a_im, 's5_log_dt': s5_log_dt,
        's5_b_re': s5_b_re, 's5_b_im': s5_b_im, 's5_c_re': s5_c_re, 's5_c_im': s5_c_im,
        's5_d': s5_d, 's5_w_glu': s5_w_glu, 'cd_w_out': cd_w_out,
        'final_norm': final_norm,
    }


def reference(x, norm_ffn1, ffn1_w_gu, ffn1_w_down, norm_mix, norm_ffn2, ffn2_w_gu, ffn2_w_down,
              ab_w_in, lru_conv_w, lru_conv_b, lru_gate_w, lru_gate_b, lru_lambda, mlstm_gate_b,
              mlstm_norm, ab_w_out, cd_w_in, gla_w_gate2, gla_gate_b, gla_norm, s5_a_re, s5_a_im,
              s5_log_dt, s5_b_re, s5_b_im, s5_c_re, s5_c_im, s5_d, s5_w_glu, cd_w_out, final_norm):
    for l in range(DEPTH):
        x = x + 0.5 * swiglu(rms_norm(x, norm_ffn1[l]), ffn1_w_gu[l], ffn1_w_down[l])
        hn = rms_norm(x, norm_mix[l])
        j = l // 2
        if l % 2 == 0:
            x = x + mixer_ab(hn, ab_w_in[j], lru_conv_w[j], lru_conv_b[j], lru_gate_w[j], lru_gate_b[j],
                             lru_lambda[j], mlstm_gate_b[j], mlstm_norm[j], ab_w_out[j])
        else:
            x = x + mixer_cd(hn, cd_w_in[j], gla_w_gate2[j], gla_gate_b[j], gla_norm[j], s5_a_re[j],
                             s5_a_im[j], s5_log_dt[j], s5_b_re[j], s5_b_im[j], s5_c_re[j], s5_c_im[j],
                             s5_d[j], s5_w_glu[j], cd_w_out[j])
        x = x + 0.5 * swiglu(rms_norm(x, norm_ffn2[l]), ffn2_w_gu[l], ffn2_w_down[l])
    return rms_norm(x, final_norm)
```

```python
from contextlib import ExitStack
import numpy as np
import concourse.bass as bass
import concourse.mybir as mybir
from concourse.bass_utils import run_bass_kernel_spmd

F32 = mybir.dt.float32
BF16 = mybir.dt.bfloat16
AF = mybir.ActivationFunctionType
ALU = mybir.AluOpType

D = 2048
DFF = 5632
NFC = DFF // 128
NDC = D // 128
TOK = 2048
SEQ = 8192
EPS = 1e-6
NCORES = 8


class Tok:
    __slots__ = ("name", "w", "r", "dsem")

    def __init__(self, name=""):
        self.name = name
        self.w = None
        self.r = []
        self.dsem = None


class Eng:
    def __init__(self, cx, key, eng, sem):
        self.cx = cx
        self.key = key
        self.eng = eng
        self.sem = sem
        self.count = 0
        self.waited = {}


class Cx:
    def __init__(self, nc, es):
        self.nc = nc
        self.es = es
        self.E = {}
        for key, eng in (("pe", nc.tensor), ("dve", nc.vector), ("act", nc.scalar),
                         ("pool", nc.gpsimd), ("sp", nc.sync)):
            sem = es.enter_context(nc.semaphore("c_" + key))
            self.E[key] = Eng(self, key, eng, sem)
        self.free_dsems = []
        self.all_dsems = []
        self.nsem = 5
        self.dram_w = []
        self.uid = 0
        self.dead = []
        self.alias = []
        self.scopes = []

    def tok(self, name=""):
        t = Tok(name)
        t.r = list(self.alias)
        if self.scopes:
            self.scopes[-1].append(t)
        return t

    def toks(self, n, name=""):
        return [self.tok("%s%d" % (name, i)) for i in range(n)]

    class _Scope:
        def __init__(self, cx):
            self.cx = cx
            self.es = ExitStack()

        def __enter__(self):
            self.cx.scopes.append([])
            self.es.__enter__()
            return self.es

        def __exit__(self, *a):
            toks = self.cx.scopes.pop()
            deps = list(self.cx.alias)
            for t in toks:
                if t.w is not None:
                    deps.append(t.w)
                deps.extend(t.r)
            best = {}
            for d in deps:
                k = id(d[0])
                if k not in best or best[k][1] < d[1]:
                    best[k] = d
            self.cx.alias = list(best.values())
            self.cx.dead.extend(toks)
            return self.es.__exit__(*a)

    def scope(self):
        return Cx._Scope(self)

    def _get_dsem(self):
        if self.free_dsems:
            return self.free_dsems.pop()
        s = self.es.enter_context(self.nc.semaphore("d%d" % self.nsem))
        self.nsem += 1
        d = [s, 0]
        self.all_dsems.append(d)
        return d

    def release(self, toks):
        for t in toks:
            if t.dsem is not None:
                self.free_dsems.append(t.dsem)
                t.dsem = None

    def _wait(self, E, dep):
        if dep is None:
            return
        sem, val, key = dep
        if key == "pe" and E.key == "pe":
            return
        if key == E.key and val > E.count:
            return
        k = id(sem)
        if E.waited.get(k, -1) >= val:
            return
        E.eng.wait_ge(sem, val)
        E.waited[k] = val

    def _deps(self, E, reads, writes):
        for t in reads:
            self._wait(E, t.w)
        for t in writes:
            if t.w is not None and t.w[2] != E.key:
                self._wait(E, t.w)
            for r in t.r:
                if r[2] != E.key:
                    self._wait(E, r)

    def _record(self, dep, reads, writes):
        for t in reads:
            t.r = [r for r in t.r if not (r[0] is dep[0])]
            t.r.append(dep)
        for t in writes:
            t.w = dep
            t.r = []

    def op(self, ek, ins_fn, reads=(), writes=(), inc=True):
        E = self.E[ek]
        self._deps(E, reads, writes)
        ins = ins_fn(E.eng)
        if inc:
            E.count += 1
            ins.then_inc(E.sem, 1)
            dep = (E.sem, E.count, ek)
        else:
            dep = (E.sem, E.count + 1, ek)
        self._record(dep, reads, writes)
        return ins

    def dma(self, qk, out, in_, reads=(), writes=(), dram_write=False, **kw):
        E = self.E[qk]
        self._deps(E, reads, writes)
        toks = list(reads) + list(writes)
        anchor = toks[0]
        if anchor.dsem is None:
            anchor.dsem = self._get_dsem()
        d = anchor.dsem
        ins = E.eng.dma_start(out=out, in_=in_, **kw)
        d[1] += 16
        ins.then_inc(d[0], 16)
        dep = (d[0], d[1], None)
        self._record(dep, reads, writes)
        if dram_write:
            self.dram_w = [x for x in self.dram_w if x[0] is not d[0]]
            self.dram_w.append(dep)
        return ins

    def barrier(self, keys=("pe", "dve", "act", "pool", "sp")):
        deps = [(E.sem, E.count, E.key) for E in self.E.values() if E.count > 0]
        deps += list(self.dram_w)
        for k in keys:
            E = self.E[k]
            for dep in deps:
                if dep[2] == k:
                    continue
                self._wait(E, dep)
        self.dram_w = []

    def phase_end(self):
        self.barrier()
        self.release(self.dead)
        self.dead = []

    def finish(self):
        E = self.E["sp"]
        for dep in list(self.dram_w):
            self._wait(E, dep)
        for o in self.E.values():
            if o.key != "sp" and o.count > 0:
                self._wait(E, (o.sem, o.count, o.key))

    def sb(self, es, shape, dt, name=None):
        self.uid += 1
        return es.enter_context(self.nc.sbuf_tensor("%s_%d" % (name or "t", self.uid), list(shape), dt))

    def ps(self, es, shape, dt, name=None):
        self.uid += 1
        return es.enter_context(self.nc.psum_tensor("%s_%d" % (name or "p", self.uid), list(shape), dt))


class Shared:
    def __init__(self, cx, es):
        nc = cx.nc
        self.ones_bf = cx.sb(es, [128, 128], BF16, "ones")
        self.t_ones = cx.tok("ones")
        cx.op("dve", lambda e: e.memset(self.ones_bf[:], 1.0), writes=[self.t_ones])
        self.eps = cx.sb(es, [128, 1], F32, "eps")
        self.t_eps = cx.tok("eps")
        cx.op("dve", lambda e: e.memset(self.eps[:], EPS), writes=[self.t_eps])
        self.bank = [cx.ps(es, [128, 512], F32, "bank%d" % i) for i in range(8)]
        self.t_bank = [cx.tok("bank%d" % i) for i in range(8)]


def emit_norm(cx, sh, es, xT, tok0, ntok, g_sb, t_g, hT, t_h, hcol0, bank_ids=(6, 7)):
    nc = cx.nc
    xs = [cx.sb(es, [128, NDC, 512], F32, "nxs") for _ in range(1)]
    t_xs = [cx.tok("xs")]
    sq = cx.sb(es, [128, NDC, 512], BF16, "nsq")
    t_sq = cx.tok("sq")
    rs = cx.sb(es, [128, 512], F32, "nrs")
    t_rs = cx.tok("rs")
    xv = xT.rearrange("(dc p) t -> p dc t", p=128)
    for si in range(ntok // 512):
        c0 = tok0 + si * 512
        b = bank_ids[si % len(bank_ids)]
        x_t, tx = xs[0], t_xs[0]
        cx.dma("sp", x_t[:], xv[:, :, c0:c0 + 512], writes=[tx])
        cx.op("act", lambda e: e.activation(out=sq[:], in_=x_t[:], func=AF.Square),
              reads=[tx], writes=[t_sq])
        for dc in range(NDC):
            cx.op("pe", lambda e: e.matmul(sh.bank[b][:], lhsT=sh.ones_bf[:], rhs=sq[:, dc, :],
                                           start=(dc == 0), stop=(dc == NDC - 1)),
                  reads=[sh.t_ones, t_sq], writes=[sh.t_bank[b]], inc=(dc == NDC - 1))
        cx.op("act", lambda e: e.activation(out=rs[:], in_=sh.bank[b][:], func=AF.Sqrt,
                                            bias=sh.eps[:], scale=1.0 / D),
              reads=[sh.t_bank[b], sh.t_eps], writes=[t_rs])
        cx.op("dve", lambda e: e.reciprocal(out=rs[:], in_=rs[:]), reads=[t_rs], writes=[t_rs])
        for dc in range(NDC):
            cx.op("dve", lambda e: e.scalar_tensor_tensor(
                out=hT[:, dc, hcol0 + si * 512: hcol0 + (si + 1) * 512], in0=x_t[:, dc, :],
                scalar=g_sb[:, dc:dc + 1], in1=rs[:], op0=ALU.mult, op1=ALU.mult),
                reads=[tx, t_g, t_rs], writes=[t_h])


def load_gain(cx, es, g_dram):
    g_sb = cx.sb(es, [128, NDC], F32, "gain")
    t_g = cx.tok("gain")
    cx.dma("sp", g_sb[:], g_dram, writes=[t_g])
    return g_sb, t_g


def emit_down(cx, sh, es, actT, t_act_list, nfc, TB, w_dram, scale, xT_in, xT_out, tok0,
              banks=(0, 1, 2, 3), act_fn=None):
    ntg = TB // 512
    nbuf = 2
    wb = [cx.sb(es, [128, nfc, 128], BF16, "dwb") for _ in range(nbuf)]
    t_wb = cx.toks(nbuf, "dwb")
    xr = [cx.sb(es, [128, TB], F32, "dxr") for _ in range(2)]
    t_xr = [cx.tok("dxr%d" % i) for i in range(2)]
    xin = xT_in.rearrange("(dc p) t -> dc p t", p=128)
    xout = xT_out.rearrange("(dc p) t -> dc p t", p=128)

    def load_w(dcn):
        cx.dma("pool", wb[dcn % nbuf][:], w_dram[dcn], writes=[t_wb[dcn % nbuf]])

    load_w(0)
    for dcn in range(NDC):
        if dcn + 1 < NDC:
            load_w(dcn + 1)
        w, tw = wb[dcn % nbuf], t_wb[dcn % nbuf]
        x_t, tx = xr[dcn % 2], t_xr[dcn % 2]
        cx.dma("sp", x_t[:], xin[dcn, :, tok0:tok0 + TB], writes=[tx])
        bs = [banks[(dcn * ntg + tg) % len(banks)] for tg in range(ntg)]
        for fc in range(nfc):
            for tg in range(ntg):
                b = bs[tg]
                if act_fn is None:
                    a_ap, a_tok = actT[:, fc, tg * 512:(tg + 1) * 512], t_act_list[fc]
                else:
                    a_ap, a_tok = act_fn(fc, tg * 512, (tg + 1) * 512)
                cx.op("pe", lambda e: e.matmul(sh.bank[b][:], lhsT=w[:, fc, :], rhs=a_ap,
                                               start=(fc == 0), stop=(fc == nfc - 1)),
                      reads=[tw, a_tok], writes=[sh.t_bank[b]], inc=(fc == nfc - 1))
        for tg in range(ntg):
            b = bs[tg]
            cx.op("dve", lambda e: e.scalar_tensor_tensor(
                out=x_t[:, tg * 512:(tg + 1) * 512], in0=sh.bank[b][:], scalar=float(scale),
                in1=x_t[:, tg * 512:(tg + 1) * 512], op0=ALU.mult, op1=ALU.add),
                reads=[sh.t_bank[b], tx], writes=[tx])
        cx.dma("sp", xout[dcn, :, tok0:tok0 + TB], x_t[:], reads=[tx], dram_write=True)


def ffn_phase(cx, sh, xT_in, xT_out, g_dram, wgu_dram, wdn_dram, ntok=TOK, TB=1024, dbg=None):
    nc = cx.nc
    with cx.scope() as es:
        g_sb, t_g = load_gain(cx, es, g_dram)
        hT = cx.sb(es, [128, NDC, TB], BF16, "hT")
        t_h = cx.tok("hT")
        actT = cx.sb(es, [128, NFC, TB], BF16, "actT")
        t_act = [cx.tok("act%d" % i) for i in range(NFC)]
        nwb = 3
        wb = [cx.sb(es, [128, NDC, 256], BF16, "gwb") for _ in range(nwb)]
        t_wb = [cx.tok("gwb%d" % i) for i in range(nwb)]
        sg = [cx.sb(es, [128, 512], F32, "sg") for _ in range(2)]
        t_sg = [cx.tok("sg%d" % i) for i in range(2)]
        ntg = TB // 512
        for blk in range(ntok // TB):
            tok0 = blk * TB
            with cx.scope() as es2:
                emit_norm(cx, sh, es2, xT_in, tok0, TB, g_sb, t_g, hT, t_h, 0)

            def load_w(j):
                cx.dma("pool", wb[j % nwb][:].rearrange("p a b -> p (a b)"), wgu_dram[j],
                       writes=[t_wb[j % nwb]])
            load_w(0)
            load_w(1)
            k = 0
            for j in range(NFC):
                if j + 2 < NFC:
                    load_w(j + 2)
                w, tw = wb[j % nwb], t_wb[j % nwb]
                for tg in range(ntg):
                    bg = (k % 2) * 2
                    bu = bg + 1
                    for half, b in ((0, bg), (1, bu)):
                        for dc in range(NDC):
                            cx.op("pe", lambda e: e.matmul(
                                sh.bank[b][:], lhsT=w[:, dc, half * 128:(half + 1) * 128],
                                rhs=hT[:, dc, tg * 512:(tg + 1) * 512],
                                start=(dc == 0), stop=(dc == NDC - 1)),
                                reads=[tw, t_h], writes=[sh.t_bank[b]], inc=(dc == NDC - 1))
                    s_t, ts = sg[k % 2], t_sg[k % 2]
                    cx.op("act", lambda e: e.activation(out=s_t[:], in_=sh.bank[bg][:], func=AF.Silu),
                          reads=[sh.t_bank[bg]], writes=[ts])
                    cx.op("dve", lambda e: e.tensor_tensor(
                        out=actT[:, j, tg * 512:(tg + 1) * 512], in0=s_t[:], in1=sh.bank[bu][:],
                        op=ALU.mult), reads=[ts, sh.t_bank[bu]], writes=[t_act[j]])
                    k += 1
            if dbg is not None and blk == 0:
                cx.dma("sp", dbg[0], hT[:], reads=[t_h], dram_write=True)
                cx.dma("sp", dbg[1], actT[:], reads=t_act, dram_write=True)
            with cx.scope() as es2:
                emit_down(cx, sh, es2, actT, t_act, NFC, TB, wdn_dram, 0.5, xT_in, xT_out, tok0,
                          banks=(4, 5, 6, 7))
    cx.phase_end()


def tile_wgu(w):
    g = w[:, :DFF].reshape(NDC, 128, NFC, 128).transpose(2, 1, 0, 3)
    u = w[:, DFF:].reshape(NDC, 128, NFC, 128).transpose(2, 1, 0, 3)
    return np.ascontiguousarray(np.concatenate([g, u], axis=-1)).reshape(NFC, 128, NDC * 256)


def tile_wdown(w, nfc, ndc=NDC):
    return np.ascontiguousarray(w.reshape(nfc, 128, ndc, 128).transpose(2, 1, 0, 3))


def load_const(cx, es, dram_ap, shape, dt=F32, q="sp", name="c"):
    t = cx.sb(es, shape, dt, name)
    tk = cx.tok(name)
    cx.dma(q, t[:], dram_ap, writes=[tk])
    return t, tk


def hn_block_ap(hn_all, tb):
    if len(hn_all.shape) == 2:
        return hn_all.rearrange("(dc p) t -> p dc t", p=128)[:, :, tb * 512:(tb + 1) * 512]
    r, c = tb // 4, (tb % 4) * 512
    return hn_all[r].rearrange("(dc p) t -> p dc t", p=128)[:, :, c:c + 512]


def proj_phase(cx, sh, hn_all, w_dram, nft, projT, wv_dram=None, vtok=None, VW=257, rep_dram=None, nrep=0):
    with cx.scope() as es:
        ntot = nft + nrep
        w = cx.sb(es, [128, ntot, NDC, 128], BF16, "pw")
        t_w = cx.tok("pw")
        for ft in range(nft):
            cx.dma("pool", w[:, ft], w_dram[ft], writes=[t_w])
        if nrep:
            rp = cx.sb(es, [128, NDC, nrep], F32, "rp")
            t_rp = cx.tok("rp")
            cx.dma("sp", rp[:], rep_dram, writes=[t_rp])
            for i in range(nrep):
                cx.op("dve", lambda e: e.tensor_copy(
                    out=w[:, nft + i], in_=rp[:, :, i:i + 1].to_broadcast([128, NDC, 128])),
                    reads=[t_rp], writes=[t_w])
        if wv_dram is not None:
            wv = cx.sb(es, [128, NDC, 256], BF16, "pwv")
            t_wv = cx.tok("pwv")
            cx.dma("pool", wv[:], wv_dram, writes=[t_wv])
            vst = [cx.sb(es, [128, 4, VW], BF16, "vst") for _ in range(2)]
            t_vst = cx.toks(2, "vst")
            for i in range(2):
                cx.op("dve", lambda e: e.memset(vst[i][:], 1.0), writes=[t_vst[i]])
            vv = vtok.rearrange("(n p) c -> p n c", p=128)
        hb = [cx.sb(es, [128, NDC, 512], BF16, "hb") for _ in range(2)]
        t_hb = cx.toks(2, "hb")
        st = [cx.sb(es, [128, ntot, 512], F32, "pst") for _ in range(2)]
        t_st = cx.toks(2, "pst")
        pv = projT.rearrange("f p t -> p f t")
        k = 0
        for tb in range(16):
            h, th = hb[tb % 2], t_hb[tb % 2]
            cx.dma("sp", h[:], hn_block_ap(hn_all, tb), writes=[th])
            s_t, ts = st[tb % 2], t_st[tb % 2]
            for ft in range(ntot):
                b = k % 4
                k += 1
                for dc in range(NDC):
                    cx.op("pe", lambda e: e.matmul(sh.bank[b][:], lhsT=w[:, ft, dc, :], rhs=h[:, dc, :],
                                                   start=(dc == 0), stop=(dc == NDC - 1)),
                          reads=[t_w, th], writes=[sh.t_bank[b]], inc=(dc == NDC - 1))
                if ft % 2 == 0:
                    cx.op("act", lambda e: e.copy(out=s_t[:, ft, :], in_=sh.bank[b][:]),
                          reads=[sh.t_bank[b]], writes=[ts])
                else:
                    cx.op("dve", lambda e: e.tensor_copy(out=s_t[:, ft, :], in_=sh.bank[b][:]),
                          reads=[sh.t_bank[b]], writes=[ts])
            cx.dma("sp", pv[:, :, tb * 512:(tb + 1) * 512], s_t[:], reads=[ts], dram_write=True)
            if wv_dram is not None:
                v_t, tv = vst[tb % 2], t_vst[tb % 2]
                for tt in range(4):
                    b = 4 + (tt % 2)
                    for dc in range(NDC):
                        cx.op("pe", lambda e: e.matmul(sh.bank[b][:, 0:256], lhsT=h[:, dc, tt * 128:(tt + 1) * 128],
                                                       rhs=wv[:, dc, :], start=(dc == 0), stop=(dc == NDC - 1)),
                              reads=[t_wv, th], writes=[sh.t_bank[b]], inc=(dc == NDC - 1))
                    cx.op("act", lambda e: e.copy(out=v_t[:, tt, 0:256], in_=sh.bank[b][:, 0:256]),
                          reads=[sh.t_bank[b]], writes=[tv])
                cx.dma("sp", vv[:, tb * 4:(tb + 1) * 4, :], v_t[:], reads=[tv], dram_write=True)
    cx.phase_end()


GELU_C = 1.5957691216057308


def emit_gelu_mul(cx, es, x_ap, t_x, other_ap, t_other, out_ap, t_out, shape, tmp=None):
    u = cx.sb(es, shape, F32, "gl_u")
    t_u = cx.tok("gl_u")
    cx.op("act", lambda e: e.activation(out=u[:], in_=x_ap, func=AF.Square), reads=[t_x], writes=[t_u])
    cx.op("dve", lambda e: e.tensor_scalar(out=u[:], in0=u[:], scalar1=0.044715, scalar2=1.0,
                                           op0=ALU.mult, op1=ALU.add), reads=[t_u], writes=[t_u])
    cx.op("dve", lambda e: e.tensor_tensor(out=u[:], in0=u[:], in1=x_ap, op=ALU.mult),
          reads=[t_u, t_x], writes=[t_u])
    cx.op("act", lambda e: e.activation(out=u[:], in_=u[:], func=AF.Sigmoid, scale=GELU_C),
          reads=[t_u], writes=[t_u])
    if other_ap is None:
        cx.op("dve", lambda e: e.tensor_tensor(out=out_ap, in0=u[:], in1=x_ap, op=ALU.mult),
              reads=[t_u, t_x], writes=[t_out])
    else:
        cx.op("dve", lambda e: e.tensor_tensor(out=u[:], in0=u[:], in1=x_ap, op=ALU.mult),
              reads=[t_u, t_x], writes=[t_u])
        cx.op("dve", lambda e: e.tensor_tensor(out=out_ap, in0=u[:], in1=other_ap, op=ALU.mult),
              reads=[t_u, t_other], writes=[t_out])


def lru_tile(cx, sh, xrT, grT, prm_dram, gw_dram, yT_rows):
    S = SEQ
    SEG = 2048
    with cx.scope() as es:
        prm, t_prm = load_const(cx, es, prm_dram, [128, 16], name="lprm")
        gw, t_gw = load_const(cx, es, gw_dram.rearrange("g i j -> i g j"), [128, 4, 128], name="lgw")
        X = cx.sb(es, [128, S], F32, "lX")
        t_X = cx.tok("lX")
        XC = cx.sb(es, [128, S], F32, "lXC")
        t_XC = cx.tok("lXC")
        cx.dma("sp", X[:], xrT, writes=[t_X])
        sp = cx.sb(es, [128, 8], F32, "lsp")
        t_sp = cx.tok("lsp")
        cx.op("act", lambda e: e.activation(out=sp[:, 0:2], in_=prm[:, 9:11], func=AF.Exp, scale=-1.0),
              reads=[t_prm], writes=[t_sp])
        cx.op("dve", lambda e: e.tensor_scalar(out=sp[:, 2:4], in0=sp[:, 0:2], scalar1=2.0, scalar2=None,
                                               op0=ALU.add), reads=[t_sp], writes=[t_sp])
        cx.op("dve", lambda e: e.reciprocal(out=sp[:, 2:4], in_=sp[:, 2:4]), reads=[t_sp], writes=[t_sp])
        cx.op("dve", lambda e: e.tensor_tensor(out=sp[:, 0:2], in0=sp[:, 0:2], in1=sp[:, 2:4], op=ALU.mult),
              reads=[t_sp], writes=[t_sp])
        cx.op("dve", lambda e: e.tensor_tensor(out=sp[:, 2:4], in0=sp[:, 0:2], in1=sp[:, 0:2], op=ALU.mult),
              reads=[t_sp], writes=[t_sp])
        cx.op("dve", lambda e: e.tensor_scalar(out=sp[:, 4:6], in0=sp[:, 2:4], scalar1=1.0 / 7, scalar2=1.0 / 5,
                                               op0=ALU.mult, op1=ALU.add), reads=[t_sp], writes=[t_sp])
        cx.op("dve", lambda e: e.tensor_tensor(out=sp[:, 4:6], in0=sp[:, 4:6], in1=sp[:, 2:4], op=ALU.mult),
              reads=[t_sp], writes=[t_sp])
        cx.op("dve", lambda e: e.tensor_scalar(out=sp[:, 4:6], in0=sp[:, 4:6], scalar1=1.0 / 3, scalar2=None,
                                               op0=ALU.add), reads=[t_sp], writes=[t_sp])
        cx.op("dve", lambda e: e.tensor_tensor(out=sp[:, 4:6], in0=sp[:, 4:6], in1=sp[:, 2:4], op=ALU.mult),
              reads=[t_sp], writes=[t_sp])
        cx.op("dve", lambda e: e.tensor_scalar(out=sp[:, 4:6], in0=sp[:, 4:6], scalar1=1.0, scalar2=None,
                                               op0=ALU.add), reads=[t_sp], writes=[t_sp])
        cx.op("dve", lambda e: e.tensor_tensor(out=sp[:, 4:6], in0=sp[:, 4:6], in1=sp[:, 0:2], op=ALU.mult),
              reads=[t_sp], writes=[t_sp])
        cx.op("dve", lambda e: e.tensor_scalar(out=sp[:, 6:8], in0=sp[:, 4:6], scalar1=-16.0, scalar2=None,
                                               op0=ALU.mult), reads=[t_sp], writes=[t_sp])
        cx.op("dve", lambda e: e.tensor_scalar(out=XC[:], in0=X[:], scalar1=prm[:, 2:3], scalar2=prm[:, 4:5],
                                               op0=ALU.mult, op1=ALU.add), reads=[t_X, t_prm], writes=[t_XC])
        for (wc, dst, src) in ((0, XC[:, 2:S], X[:, 0:S - 2]), (1, XC[:, 1:S], X[:, 0:S - 1]),
                               (3, XC[:, 0:S - 1], X[:, 1:S])):
            cx.op("dve", lambda e: e.scalar_tensor_tensor(out=dst, in0=src, scalar=prm[:, wc:wc + 1], in1=dst,
                                                          op0=ALU.mult, op1=ALU.add),
                  reads=[t_X, t_prm, t_XC], writes=[t_XC])
        HF = X
        t_HF = t_X
        R = [cx.sb(es, [128, SEG], F32, "lR") for _ in range(2)]
        t_R = cx.toks(2, "lR")
        I = [cx.sb(es, [128, SEG], F32, "lI") for _ in range(2)]
        t_I = cx.toks(2, "lI")
        A2 = [cx.sb(es, [128, SEG], F32, "lA2") for _ in range(2)]
        t_A2 = cx.toks(2, "lA2")
        GR = cx.sb(es, [128, SEG], F32, "lGR")
        t_GR = cx.tok("lGR")
        YB = cx.sb(es, [128, SEG], BF16, "lYB")
        t_YB = cx.tok("lYB")
        carry = cx.sb(es, [128, 2], F32, "lcarry")
        t_carry = cx.tok("lcarry")
        cx.op("dve", lambda e: e.memset(carry[:], 0.0), writes=[t_carry])
        nseg = S // SEG
        kk = 0
        for d in (0, 1):
            segs = range(nseg) if d == 0 else range(nseg - 1, -1, -1)
            for sg in segs:
                c0 = sg * SEG
                r_t, tr = R[kk % 2], t_R[kk % 2]
                i_t, ti = I[kk % 2], t_I[kk % 2]
                a2, ta2 = A2[kk % 2], t_A2[kk % 2]
                kk += 1
                for gi, (dst, tdst) in enumerate(((r_t, tr), (i_t, ti))):
                    gidx = d * 2 + gi
                    for q in range(SEG // 512):
                        b = (q + gi * 4) % 4
                        cx.op("pe", lambda e: e.matmul(sh.bank[b][:], lhsT=gw[:, gidx, :],
                                                       rhs=XC[:, c0 + q * 512: c0 + (q + 1) * 512],
                                                       start=True, stop=True),
                              reads=[t_gw, t_XC], writes=[sh.t_bank[b]])
                        cx.op("act", lambda e: e.activation(out=dst[:, q * 512:(q + 1) * 512], in_=sh.bank[b][:],
                                                            func=AF.Sigmoid, bias=prm[:, 5 + gidx:6 + gidx]),
                              reads=[sh.t_bank[b], t_prm], writes=[tdst])
                cx.op("act", lambda e: e.activation(out=r_t[:], in_=r_t[:], func=AF.Exp, scale=sp[:, 6 + d:7 + d]),
                      reads=[tr, t_sp], writes=[tr])
                cx.op("dve", lambda e: e.tensor_tensor(out=i_t[:], in0=i_t[:], in1=XC[:, c0:c0 + SEG], op=ALU.mult),
                      reads=[ti, t_XC], writes=[ti])
                cx.op("dve", lambda e: e.tensor_tensor(out=a2[:], in0=r_t[:], in1=r_t[:], op=ALU.mult),
                      reads=[tr], writes=[ta2])
                cx.op("act", lambda e: e.activation(out=a2[:], in_=a2[:], func=AF.Sqrt, scale=-1.0, bias=1.0),
                      reads=[ta2], writes=[ta2])
                cx.op("dve", lambda e: e.tensor_tensor(out=i_t[:], in0=i_t[:], in1=a2[:], op=ALU.mult),
                      reads=[ti, ta2], writes=[ti])
                if d == 0:
                    cx.op("dve", lambda e: e.tensor_tensor_scan(
                        out=HF[:, c0:c0 + SEG], data0=r_t[:], data1=i_t[:], initial=carry[:, 0:1],
                        op0=ALU.mult, op1=ALU.add), reads=[tr, ti, t_carry], writes=[t_HF])
                    cx.op("dve", lambda e: e.tensor_copy(out=carry[:, 0:1], in_=HF[:, c0 + SEG - 1:c0 + SEG]),
                          reads=[t_HF], writes=[t_carry])
                else:
                    cx.op("dve", lambda e: e.tensor_tensor_scan(
                        out=a2[:, ::-1], data0=r_t[:, ::-1], data1=i_t[:, ::-1], initial=carry[:, 1:2],
                        op0=ALU.mult, op1=ALU.add), reads=[tr, ti, t_carry], writes=[ta2])
                    cx.op("dve", lambda e: e.tensor_copy(out=carry[:, 1:2], in_=a2[:, 0:1]),
                          reads=[ta2], writes=[t_carry])
                    cx.op("dve", lambda e: e.tensor_tensor(out=a2[:], in0=a2[:], in1=HF[:, c0:c0 + SEG], op=ALU.add),
                          reads=[ta2, t_HF], writes=[ta2])
                    cx.dma("sp", GR[:], grT[:, c0:c0 + SEG], writes=[t_GR])
                    with cx.scope() as es2:
                        emit_gelu_mul(cx, es2, GR[:], t_GR, a2[:], ta2, YB[:], t_YB, [128, SEG])
                    cx.dma("sp", yT_rows[:, c0:c0 + SEG], YB[:], reads=[t_YB], dram_write=True)


def tile_cols(W, starts):
    return np.ascontiguousarray(np.stack(
        [W[:, s:s + 128].reshape(NDC, 128, 128).transpose(1, 0, 2) for s in starts]))


def tile_wide(W, start, width):
    return np.ascontiguousarray(W[:, start:start + width].reshape(NDC, 128, width).transpose(1, 0, 2))


def ab_host_tiles(P, hq):
    W = P["ab_w_in"][0]
    starts = [256 * hq, 256 * hq + 128, 1024 + 256 * hq, 1024 + 256 * hq + 128,
              2048 + 256 * hq, 2048 + 256 * hq + 128, 3072 + 256 * hq, 3072 + 256 * hq + 128,
              5120 + 256 * hq, 5120 + 256 * hq + 128]
    out = {"ab_w": tile_cols(W, starts), "ab_wv": tile_wide(W, 4096 + 256 * hq, 256)}
    gcols = [6144 + d * 8 + g * 4 + hq for d in (0, 1) for g in (0, 1)]
    out["ab_rep"] = np.ascontiguousarray(W[:, gcols].reshape(NDC, 128, 4).transpose(1, 0, 2))
    prm = np.zeros((2, 128, 16), np.float32)
    gw = np.zeros((2, 4, 128, 128), np.float32)
    for i in range(2):
        hh = 2 * hq + i
        ch = slice(128 * hh, 128 * hh + 128)
        prm[i, :, 0:4] = P["lru_conv_w"][0][:, ch].T
        prm[i, :, 4] = P["lru_conv_b"][0][ch]
        for d in (0, 1):
            for g in (0, 1):
                prm[i, :, 5 + d * 2 + g] = P["lru_gate_b"][0][d, g, ch]
                gw[i, d * 2 + g] = P["lru_gate_w"][0][d, g, hh]
            prm[i, :, 9 + d] = P["lru_lambda"][0][d, ch]
    out["lru_prm"] = prm
    out["lru_gw"] = gw
    return out


L = 128
NCH = SEQ // L
SBW = 1024
NSB = SEQ // SBW
CPS = SBW // L


class CLAConsts:
    def __init__(self, cx, es, cmask_dram, ident_dram):
        self.mask, self.t_mask = load_const(cx, es, cmask_dram, [128, 256], name="cmask")
        idf, t_idf = load_const(cx, es, ident_dram, [128, 128], name="identf")
        self.ident = cx.sb(es, [128, 128], BF16, "ident")
        self.t_ident = cx.tok("ident")
        cx.op("dve", lambda e: e.tensor_copy(out=self.ident[:], in_=idf[:]), reads=[t_idf], writes=[self.t_ident])
        self.mf = cx.sb(es, [128, SBW], F32, "mf")
        self.mb = cx.sb(es, [128, SBW], F32, "mb")
        self.t_m = cx.tok("mreset")
        cx.op("dve", lambda e: e.memset(self.mf[:], 1.0), writes=[self.t_m])
        cx.op("dve", lambda e: e.memset(self.mb[:], 1.0), writes=[self.t_m])
        cx.op("dve", lambda e: e.memset(self.mf[:, 0:SBW:L], 0.0), writes=[self.t_m])
        cx.op("dve", lambda e: e.memset(self.mb[:, L - 1:SBW:L], 0.0), writes=[self.t_m])


def cla_run(cx, sh, cc, cfg):
    ndt = cfg["ndt"]
    VW = cfg["VW"]
    sc = cfg["sc"]
    normalise = cfg["normalise"]
    with cx.scope() as es:
        HF = cx.sb(es, [128, NCH, 256], F32, "HF")
        t_HF = cx.tok("HF")
        S = cx.sb(es, [128, ndt, VW], F32, "S")
        Sb = cx.sb(es, [128, ndt, VW], BF16, "Sb")
        t_S = cx.tok("S")
        t_Sb = cx.tok("Sb")
        ld = [cx.sb(es, [128, SBW], F32, "cld") for _ in range(2)]
        t_ld = cx.toks(2, "cld")
        nl = [cx.sb(es, [128, SBW], F32, "nl") for _ in range(ndt)]
        t_nl = cx.toks(ndt, "nl")
        ci = [cx.sb(es, [128, SBW], F32, "ci") for _ in range(ndt)]
        t_ci = cx.toks(ndt, "ci")
        ex = cx.sb(es, [128, SBW], F32, "ex")
        t_ex = cx.tok("ex")
        totn = cx.sb(es, [128, ndt, CPS], F32, "totn")
        t_totn = cx.tok("totn")
        g = cx.sb(es, [128, ndt, CPS], F32, "g")
        t_g = cx.tok("g")
        qs = cx.sb(es, [128, ndt, SBW], BF16, "qs")
        ks = cx.sb(es, [128, ndt, SBW], BF16, "ks")
        kh = cx.sb(es, [128, ndt, SBW], BF16, "khT")
        t_qs, t_ks, t_khT = cx.tok("qs"), cx.tok("ks"), cx.tok("khT")
        ktok = cx.sb(es, [128, CPS, ndt * 128], BF16, "ktok")
        t_ktok = cx.tok("ktok")
        vt = cx.sb(es, [128, CPS, VW], BF16, "vt")
        t_vt = cx.tok("vt")
        pT = [cx.sb(es, [128, 128], BF16, "pT") for _ in range(2)]
        t_pT = cx.toks(2, "pT")
        vs = cx.sb(es, [128, 8], F32, "vsm")
        t_vs = cx.tok("vsm")
        gt = cx.sb(es, [128, 2, SBW], F32, "gateT")
        t_gt = cx.tok("gateT")
        hb = [cx.sb(es, [128, 256], F32, "hsum") for _ in range(2)]
        t_hb = cx.toks(2, "hsum")
        hnb = [cx.sb(es, [128, 256], BF16, "hnb") for _ in range(2)]
        t_hnb = cx.toks(2, "hnb")
        sqj = cx.sb(es, [128, 256], F32, "sqj")
        t_sqj = cx.tok("sqj")
        YT = cx.sb(es, [128, 2, SBW], BF16, "YT")
        t_YT = cx.tok("YT")
        vv = cfg["vtok"].rearrange("(n p) c -> p n c", p=128)
        kcount = 0
        for d in (0, 1):
            mres = cc.mf if d == 0 else cc.mb
            moff = 0 if d == 0 else 128
            cx.op("dve", lambda e: e.memset(S[:], 0.0), writes=[t_S])
            cx.op("dve", lambda e: e.memset(Sb[:], 0.0), writes=[t_Sb])
            sbs = range(NSB) if d == 0 else range(NSB - 1, -1, -1)
            for sbi in sbs:
                c0 = sbi * SBW
                cfg["gate_src"](cx, es, d, c0, nl, t_nl)
                for dt in range(ndt):
                    if d == 0:
                        cx.op("dve", lambda e: e.tensor_tensor_scan(
                            out=ci[dt][:], data0=mres[:], data1=nl[dt][:], initial=0.0,
                            op0=ALU.mult, op1=ALU.add), reads=[cc.t_m, t_nl[dt]], writes=[t_ci[dt]])
                        tcol = ci[dt][:, L - 1:SBW:L]
                    else:
                        cx.op("dve", lambda e: e.tensor_tensor_scan(
                            out=ci[dt][:, ::-1], data0=mres[:, ::-1], data1=nl[dt][:, ::-1], initial=0.0,
                            op0=ALU.mult, op1=ALU.add), reads=[cc.t_m, t_nl[dt]], writes=[t_ci[dt]])
                        tcol = ci[dt][:, 0:SBW:L]
                    cx.op("dve", lambda e: e.tensor_copy(out=totn[:, dt, :], in_=tcol),
                          reads=[t_ci[dt]], writes=[t_totn])
                    cx.op("act", lambda e: e.activation(out=g[:, dt, :], in_=totn[:, dt, :], func=AF.Exp, scale=-sc),
                          reads=[t_totn], writes=[t_g])
                    cx.op("act", lambda e: e.activation(out=ex[:], in_=ci[dt][:], func=AF.Exp, scale=-sc),
                          reads=[t_ci[dt]], writes=[t_ex])
                    l_t, tl = ld[kcount % 2], t_ld[kcount % 2]
                    kcount += 1
                    cx.dma("sp", l_t[:], cfg["qT"][dt][:, c0:c0 + SBW], writes=[tl])
                    cx.op("dve", lambda e: e.scalar_tensor_tensor(
                        out=qs[:, dt, :], in0=l_t[:], scalar=float(cfg["qscale"]), in1=ex[:],
                        op0=ALU.mult, op1=ALU.mult), reads=[tl, t_ex], writes=[t_qs])
                    if cfg["ipre"] is not None:
                        l_t, tl = ld[kcount % 2], t_ld[kcount % 2]
                        kcount += 1
                        cx.dma("sp", l_t[:], cfg["ipre"][d][:, c0:c0 + SBW], writes=[tl])
                        cx.op("dve", lambda e: e.tensor_tensor(out=ci[dt][:], in0=ci[dt][:], in1=l_t[:], op=ALU.add),
                              reads=[t_ci[dt], tl], writes=[t_ci[dt]])
                        ib = cfg["ibias"][d]
                    else:
                        ib = 0.0
                    l_t, tl = ld[kcount % 2], t_ld[kcount % 2]
                    kcount += 1
                    cx.dma("sp", l_t[:], cfg["kT"][dt][:, c0:c0 + SBW], writes=[tl])
                    cx.op("act", lambda e: e.activation(out=ex[:], in_=ci[dt][:], func=AF.Exp, scale=sc, bias=ib),
                          reads=[t_ci[dt]] + ([cfg["t_prm"]] if cfg["ipre"] is not None else []), writes=[t_ex])
                    cx.op("dve", lambda e: e.tensor_tensor(out=ks[:, dt, :], in0=l_t[:], in1=ex[:], op=ALU.mult),
                          reads=[tl, t_ex], writes=[t_ks])
                    cx.op("dve", lambda e: e.tensor_tensor(
                        out=ci[dt][:].rearrange("p (c l) -> p c l", l=L),
                        in0=ci[dt][:].rearrange("p (c l) -> p c l", l=L),
                        in1=totn[:, dt, :].unsqueeze(2).to_broadcast([128, CPS, L]), op=ALU.subtract),
                        reads=[t_ci[dt], t_totn], writes=[t_ci[dt]])
                    cx.op("act", lambda e: e.activation(out=ex[:], in_=ci[dt][:], func=AF.Exp, scale=sc, bias=ib),
                          reads=[t_ci[dt]] + ([cfg["t_prm"]] if cfg["ipre"] is not None else []), writes=[t_ex])
                    cx.op("dve", lambda e: e.tensor_tensor(out=kh[:, dt, :], in0=l_t[:], in1=ex[:], op=ALU.mult),
                          reads=[tl, t_ex], writes=[t_khT])
                for c in range(CPS):
                    b = 4 + (c % 2)
                    bv = sh.bank[b][:].bitcast(BF16)
                    for dt in range(ndt):
                        cx.op("pe", lambda e: e.transpose(out=bv[:, dt * 128:(dt + 1) * 128],
                                                          in_=kh[:, dt, c * L:(c + 1) * L], identity=cc.ident[:]),
                              reads=[t_khT, cc.t_ident], writes=[sh.t_bank[b]], inc=(dt == ndt - 1))
                    cx.op("act", lambda e: e.copy(out=ktok[:, c, :], in_=bv[:, 0:ndt * 128]),
                          reads=[sh.t_bank[b]], writes=[t_ktok])
                cx.dma("sp", vt[:], vv[:, sbi * CPS:(sbi + 1) * CPS, :], writes=[t_vt])
                if d == 1:
                    for et in range(2):
                        l_t, tl = ld[kcount % 2], t_ld[kcount % 2]
                        kcount += 1
                        cx.dma("sp", l_t[:], cfg["gateT"][et][:, c0:c0 + SBW], writes=[tl])
                        if cfg["gate_fn"] == "sigmoid":
                            cx.op("act", lambda e: e.activation(out=gt[:, et, :], in_=l_t[:], func=AF.Sigmoid),
                                  reads=[tl], writes=[t_gt])
                        else:
                            cx.op("act", lambda e: e.activation(out=gt[:, et, :], in_=l_t[:], func=AF.Silu),
                                  reads=[tl], writes=[t_gt])
                chs = range(CPS) if d == 0 else range(CPS - 1, -1, -1)
                for c in chs:
                    gc = sbi * CPS + c
                    cs = slice(c * L, (c + 1) * L)
                    cx_b = 0
                    for dt in range(ndt):
                        cx.op("pe", lambda e: e.matmul(sh.bank[0][:, 0:L], lhsT=ks[:, dt, cs], rhs=qs[:, dt, cs],
                                                       start=(dt == 0), stop=(dt == ndt - 1)),
                              reads=[t_ks, t_qs], writes=[sh.t_bank[0]], inc=(dt == ndt - 1))
                    p_t, tp = pT[gc % 2], t_pT[gc % 2]
                    cx.op("dve", lambda e: e.tensor_tensor(out=p_t[:], in0=sh.bank[0][:, 0:L],
                                                           in1=cc.mask[:, moff:moff + L], op=ALU.mult),
                          reads=[sh.t_bank[0], cc.t_mask], writes=[tp])
                    cx.op("pe", lambda e: e.matmul(sh.bank[1][:, 0:VW], lhsT=p_t[:], rhs=vt[:, c, :],
                                                   start=True, stop=False),
                          reads=[tp, t_vt], writes=[sh.t_bank[1]], inc=False)
                    for dt in range(ndt):
                        cx.op("pe", lambda e: e.matmul(sh.bank[1][:, 0:VW], lhsT=qs[:, dt, cs], rhs=Sb[:, dt, :],
                                                       start=False, stop=(dt == ndt - 1)),
                              reads=[t_qs, t_Sb], writes=[sh.t_bank[1]], inc=(dt == ndt - 1))
                    for dt in range(ndt):
                        cx.op("pe", lambda e: e.matmul(sh.bank[2 + dt][:, 0:VW], lhsT=ktok[:, c, dt * 128:(dt + 1) * 128],
                                                       rhs=vt[:, c, :], start=True, stop=True),
                              reads=[t_ktok, t_vt], writes=[sh.t_bank[2 + dt]])
                    for dt in range(ndt):
                        cx.op("dve", lambda e: e.scalar_tensor_tensor(
                            out=S[:, dt, :], in0=S[:, dt, :], scalar=g[:, dt, c:c + 1], in1=sh.bank[2 + dt][:, 0:VW],
                            op0=ALU.mult, op1=ALU.add), reads=[t_S, t_g, sh.t_bank[2 + dt]], writes=[t_S])
                    cx.op("act", lambda e: e.copy(out=Sb[:], in_=S[:]), reads=[t_S], writes=[t_Sb])
                    if normalise:
                        cx.op("dve", lambda e: e.tensor_scalar(out=vs[:, 5:6], in0=sh.bank[1][:, 256:257], scalar1=-1.0,
                                                               scalar2=None, op0=ALU.mult),
                              reads=[sh.t_bank[1]], writes=[t_vs])
                        cx.op("dve", lambda e: e.tensor_tensor(out=vs[:, 0:1], in0=sh.bank[1][:, 256:257], in1=vs[:, 5:6],
                                                               op=ALU.max), reads=[sh.t_bank[1], t_vs], writes=[t_vs])
                        cx.op("dve", lambda e: e.tensor_scalar(out=vs[:, 0:1], in0=vs[:, 0:1], scalar1=1.0,
                                                               scalar2=None, op0=ALU.max), reads=[t_vs], writes=[t_vs])
                        cx.op("dve", lambda e: e.reciprocal(out=vs[:, 1:2], in_=vs[:, 0:1]), reads=[t_vs], writes=[t_vs])
                    if d == 0:
                        if normalise:
                            cx.op("dve", lambda e: e.tensor_scalar(out=HF[:, gc, :], in0=sh.bank[1][:, 0:256],
                                                                   scalar1=vs[:, 1:2], scalar2=None, op0=ALU.mult),
                                  reads=[sh.t_bank[1], t_vs], writes=[t_HF])
                        else:
                            cx.op("act", lambda e: e.copy(out=HF[:, gc, :], in_=sh.bank[1][:, 0:256]),
                                  reads=[sh.t_bank[1]], writes=[t_HF])
                    else:
                        h_t, th = hb[gc % 2], t_hb[gc % 2]
                        hn_t, thn = hnb[gc % 2], t_hnb[gc % 2]
                        scal = vs[:, 1:2] if normalise else 1.0
                        cx.op("dve", lambda e: e.scalar_tensor_tensor(
                            out=h_t[:], in0=sh.bank[1][:, 0:256], scalar=scal, in1=HF[:, gc, :],
                            op0=ALU.mult, op1=ALU.add), reads=[sh.t_bank[1], t_vs, t_HF], writes=[th])
                        cx.op("act", lambda e: e.activation(out=sqj[:], in_=h_t[:], func=AF.Square, accum_out=vs[:, 2:3]),
                              reads=[th], writes=[t_sqj, t_vs])
                        cx.op("act", lambda e: e.activation(out=vs[:, 3:4], in_=vs[:, 2:3], func=AF.Sqrt,
                                                            scale=1.0 / 256, bias=sh.eps[:]),
                              reads=[t_vs, sh.t_eps], writes=[t_vs])
                        cx.op("dve", lambda e: e.reciprocal(out=vs[:, 4:5], in_=vs[:, 3:4]), reads=[t_vs], writes=[t_vs])
                        cx.op("dve", lambda e: e.tensor_scalar(out=hn_t[:], in0=h_t[:], scalar1=vs[:, 4:5], scalar2=None,
                                                               op0=ALU.mult), reads=[th, t_vs], writes=[thn])
                        b = 6 + (gc % 2)
                        bv = sh.bank[b][:].bitcast(BF16)
                        for et in range(2):
                            cx.op("pe", lambda e: e.transpose(out=bv[:, et * 128:(et + 1) * 128],
                                                              in_=hn_t[:, et * 128:(et + 1) * 128], identity=cc.ident[:]),
                                  reads=[thn, cc.t_ident], writes=[sh.t_bank[b]], inc=(et == 1))
                        for et in range(2):
                            cx.op("dve", lambda e: e.scalar_tensor_tensor(
                                out=YT[:, et, cs], in0=bv[:, et * 128:(et + 1) * 128], scalar=cfg["gain"][:, et:et + 1],
                                in1=gt[:, et, cs], op0=ALU.mult, op1=ALU.mult),
                                reads=[sh.t_bank[b], cfg["t_gain"], t_gt], writes=[t_YT])
                if d == 1:
                    cx.dma("sp", cfg["yT_rows"].rearrange("(e p) t -> p e t", p=128)[:, :, c0:c0 + SBW], YT[:],
                           reads=[t_YT], dram_write=True)


def mlstm_phase(cx, sh, cc, projT, vtok, mprm_dram, yT_rows):
    with cx.scope() as es:
        prm, t_prm = load_const(cx, es, mprm_dram, [128, 8], name="mprm")
        nb = cx.sb(es, [128, 4], F32, "mnb")
        t_nb = cx.tok("mnb")
        cx.op("dve", lambda e: e.tensor_scalar(out=nb[:], in0=prm[:, 0:4], scalar1=-1.0, scalar2=None, op0=ALU.mult),
              reads=[t_prm], writes=[t_nb])
        fl = cx.sb(es, [128, SBW], F32, "mfl")
        t_fl = cx.tok("mfl")

        def gate_src(cx, es_, d, c0, nl, t_nl):
            cx.dma("sp", fl[:], projT[11 + 2 * d][:, c0:c0 + SBW], writes=[t_fl])
            cx.op("act", lambda e: e.activation(out=nl[0][:], in_=fl[:], func=AF.Exp, scale=-1.0,
                                                bias=nb[:, 1 + 2 * d:2 + 2 * d]),
                  reads=[t_fl, t_nb], writes=[t_nl[0]])
            cx.op("act", lambda e: e.activation(out=nl[0][:], in_=nl[0][:], func=AF.Ln, bias=1.0),
                  reads=[t_nl[0]], writes=[t_nl[0]])
            cx.op("dve", lambda e: e.tensor_copy(out=nl[1][:], in_=nl[0][:]), reads=[t_nl[0]], writes=[t_nl[1]])

        cfg = dict(ndt=2, VW=257, sc=1.0, qscale=256 ** -0.5, normalise=True,
                   qT=[projT[4], projT[5]], kT=[projT[6], projT[7]], vtok=vtok,
                   gate_src=gate_src, ipre=[projT[10], projT[12]], ibias=[prm[:, 0:1], prm[:, 2:3]], t_prm=t_prm,
                   gateT=[projT[8], projT[9]], gate_fn="sigmoid", gain=prm[:, 4:6], t_gain=t_prm, yT_rows=yT_rows)
        cla_run(cx, sh, cc, cfg)


def gla_phase(cx, sh, cc, projT, vtok, gprm_dram, wg2_dram, yT_rows):
    with cx.scope() as es:
        prm, t_prm = load_const(cx, es, gprm_dram, [128, 8], name="gprm")
        wg2, t_wg2 = load_const(cx, es, wg2_dram, [48, 128], name="wg2")
        nb = cx.sb(es, [128, 2], F32, "gnb")
        t_nb = cx.tok("gnb")
        cx.op("dve", lambda e: e.tensor_scalar(out=nb[:], in0=prm[:, 0:2], scalar1=-1.0, scalar2=None, op0=ALU.mult),
              reads=[t_prm], writes=[t_nb])
        low = cx.sb(es, [48, SBW], F32, "glow")
        t_low = cx.tok("glow")

        def gate_src(cx, es_, d, c0, nl, t_nl):
            cx.dma("sp", low[:], projT[4][0:48, c0:c0 + SBW], writes=[t_low])
            r0 = 32 * d
            for q in range(SBW // 512):
                b = 2 + (q % 2)
                cx.op("pe", lambda e: e.matmul(sh.bank[b][:], lhsT=wg2[r0:r0 + 16, :],
                                               rhs=low[r0:r0 + 16, q * 512:(q + 1) * 512], start=True, stop=True),
                      reads=[t_wg2, t_low], writes=[sh.t_bank[b]])
                cx.op("act", lambda e: e.activation(out=nl[0][:, q * 512:(q + 1) * 512], in_=sh.bank[b][:],
                                                    func=AF.Exp, scale=-1.0, bias=nb[:, d:d + 1]),
                      reads=[sh.t_bank[b], t_nb], writes=[t_nl[0]])
            cx.op("act", lambda e: e.activation(out=nl[0][:], in_=nl[0][:], func=AF.Ln, bias=1.0),
                  reads=[t_nl[0]], writes=[t_nl[0]])

        cfg = dict(ndt=1, VW=256, sc=1.0 / 16, qscale=128 ** -0.5, normalise=False,
                   qT=[projT[0]], kT=[projT[1]], vtok=vtok,
                   gate_src=gate_src, ipre=None, ibias=None, t_prm=t_prm,
                   gateT=[projT[2], projT[3]], gate_fn="silu", gain=prm[:, 2:4], t_gain=t_prm, yT_rows=yT_rows)
        cla_run(cx, sh, cc, cfg)


TWO_PI = 6.283185307179586
S5W = 1024


def emit_sincos(cx, es, x_ap, t_x, sin_ap, cos_ap, t_out, shape):
    ki = cx.sb(es, shape, mybir.dt.int32, "sc_ki")
    kf = cx.sb(es, shape, F32, "sc_kf")
    t_k = cx.tok("sc_k")
    cx.op("dve", lambda e: e.tensor_copy(out=ki[:], in_=x_ap), reads=[t_x], writes=[t_k])
    cx.op("dve", lambda e: e.tensor_copy(out=kf[:], in_=ki[:]), reads=[t_k], writes=[t_k])
    cx.op("dve", lambda e: e.tensor_tensor(out=x_ap, in0=x_ap, in1=kf[:], op=ALU.subtract), reads=[t_x, t_k], writes=[t_x])
    cx.op("dve", lambda e: e.scalar_tensor_tensor(out=kf[:], in0=x_ap, scalar=0.5, in1=x_ap, op0=ALU.is_gt, op1=ALU.subtract),
          reads=[t_x], writes=[t_k])
    cx.op("dve", lambda e: e.scalar_tensor_tensor(out=x_ap, in0=kf[:], scalar=0.5, in1=kf[:], op0=ALU.is_gt, op1=ALU.subtract),
          reads=[t_k], writes=[t_x])
    cx.op("act", lambda e: e.activation(out=sin_ap, in_=x_ap, func=AF.Sin, scale=TWO_PI), reads=[t_x], writes=[t_out])
    cx.op("dve", lambda e: e.tensor_scalar(out=kf[:], in0=x_ap, scalar1=0.25, scalar2=None, op0=ALU.add), reads=[t_x], writes=[t_k])
    cx.op("dve", lambda e: e.scalar_tensor_tensor(out=kf[:], in0=kf[:], scalar=0.5, in1=kf[:], op0=ALU.is_gt, op1=ALU.subtract),
          reads=[t_k], writes=[t_k])
    cx.op("act", lambda e: e.activation(out=cos_ap, in_=kf[:], func=AF.Sin, scale=-TWO_PI), reads=[t_k], writes=[t_out])


def s5_phase(cx, sh, cc_identf, uT, s5p, s5b, s5c, s5d, yT_rows):
    identf, t_identf = cc_identf
    W = S5W
    nseg = SEQ // W
    with cx.scope() as es:
        ub = cx.sb(es, [128, 2, SEQ], BF16, "s5ub")
        t_ub = cx.tok("s5ub")
        Y = cx.sb(es, [128, 2, SEQ], F32, "s5Y")
        t_Y = cx.tok("s5Y")
        dch = cx.sb(es, [128, 2], F32, "s5d")
        t_dch = cx.tok("s5d")
        for ct_ in range(2):
            cx.dma("sp", dch[:, ct_:ct_ + 1], s5d[ct_], writes=[t_dch])
        negpi = cx.sb(es, [128, 1], F32, "negpi")
        t_negpi = cx.tok("negpi")
        cx.op("dve", lambda e: e.memset(negpi[:], -3.141592653589793), writes=[t_negpi])
        with cx.scope() as es2:
            uf = cx.sb(es2, [128, SEQ], F32, "s5uf")
            t_uf = cx.tok("s5uf")
            for ct in range(2):
                cx.dma("sp", uf[:], uT[ct], writes=[t_uf])
                cx.op("dve", lambda e: e.tensor_scalar(out=Y[:, ct, :], in0=uf[:], scalar1=dch[:, ct:ct + 1], scalar2=None,
                                                       op0=ALU.mult), reads=[t_uf, t_dch], writes=[t_Y])
                cx.op("act", lambda e: e.copy(out=ub[:, ct, :], in_=uf[:]), reads=[t_uf], writes=[t_ub])
        ioi = cx.sb(es, [128, W], mybir.dt.int32, "ioi")
        t_ioi = cx.tok("ioi")
        IOT = cx.sb(es, [128, 2, W], F32, "iot")
        t_IOT = cx.tok("iot")
        cx.op("pool", lambda e: e.iota(ioi[:], pattern=[[1, W]], base=0, channel_multiplier=0), writes=[t_ioi])
        cx.op("dve", lambda e: e.tensor_copy(out=IOT[:, 0, :], in_=ioi[:]), reads=[t_ioi], writes=[t_IOT])
        cx.op("dve", lambda e: e.tensor_scalar(out=IOT[:, 1, :], in0=IOT[:, 0, :], scalar1=-1.0, scalar2=float(W - 1),
                                               op0=ALU.mult, op1=ALU.add), reads=[t_IOT], writes=[t_IOT])
        prm = cx.sb(es, [128, 4], F32, "s5prm")
        t_prm = cx.tok("s5prm")
        bt = cx.sb(es, [128, 2, 16], F32, "s5bt")
        t_bt = cx.tok("s5bt")
        ctl = cx.sb(es, [128, 2, 16], F32, "s5ct")
        t_ctl = cx.tok("s5ct")
        sm = cx.sb(es, [128, 32], F32, "s5sm")
        t_sm = cx.tok("s5sm")
        bb = cx.sb(es, [128, 2, 16], F32, "s5bb")
        t_bb = cx.tok("s5bb")
        BP = cx.sb(es, [128, 2, 128], F32, "s5BP")
        t_BP = cx.tok("s5BP")
        LB = cx.sb(es, [128, 2, 128], BF16, "s5LB")
        t_LB = cx.tok("s5LB")
        CP = cx.sb(es, [128, 2, 128], BF16, "s5CP")
        t_CP = cx.tok("s5CP")
        CN = cx.sb(es, [128, W], F32, "s5CN")
        SN = cx.sb(es, [128, W], F32, "s5SN")
        t_tab = cx.tok("s5tab")
        CR = cx.sb(es, [128, W], F32, "s5CR")
        CI = cx.sb(es, [128, W], F32, "s5CI")
        t_CR, t_CI = cx.tok("s5CR"), cx.tok("s5CI")
        VR = cx.sb(es, [128, W], F32, "s5VR")
        VI = cx.sb(es, [128, W], F32, "s5VI")
        t_VR, t_VI = cx.tok("s5VR"), cx.tok("s5VI")
        T1 = cx.sb(es, [128, W], F32, "s5T1")
        T2 = cx.sb(es, [128, W], F32, "s5T2")
        t_T1, t_T2 = cx.tok("s5T1"), cx.tok("s5T2")
        XRb = cx.sb(es, [128, W], BF16, "s5XR")
        XIb = cx.sb(es, [128, W], BF16, "s5XI")
        t_XR, t_XI = cx.tok("s5XR"), cx.tok("s5XI")
        tmpa = cx.sb(es, [128, 512], F32, "s5ta")
        tmpb = cx.sb(es, [128, 512], F32, "s5tb")
        t_ta, t_tb = cx.tok("s5ta"), cx.tok("s5tb")
        carry = cx.sb(es, [128, 4], F32, "s5carry")
        t_carry = cx.tok("s5carry")

        def sm_op(fn, **kw):
            cx.op("dve", fn, reads=[t_sm] + kw.get("reads", []), writes=[t_sm])

        for d in (0, 1):
            for st in range(8):
                ct = st // 4
                go = 2 * (st % 4)
                cx.dma("sp", prm[:], s5p[d, st], writes=[t_prm])
                cx.dma("sp", bt[:], s5b[st], writes=[t_bt])
                cx.dma("sp", ctl[:], s5c[st], writes=[t_ctl])
                cx.op("act", lambda e: e.activation(out=sm[:, 0:1], in_=prm[:, 2:3], func=AF.Exp), reads=[t_prm], writes=[t_sm])
                sm_op(lambda e: e.tensor_tensor(out=sm[:, 1:3], in0=prm[:, 0:2], in1=sm[:, 0:1].to_broadcast([128, 2]), op=ALU.mult),
                      reads=[t_prm])
                cx.op("act", lambda e: e.activation(out=sm[:, 3:4], in_=sm[:, 1:2], func=AF.Exp), reads=[t_sm], writes=[t_sm])
                sm_op(lambda e: e.tensor_scalar(out=sm[:, 4:5], in0=sm[:, 2:3], scalar1=1.0 / TWO_PI, scalar2=None, op0=ALU.mult))
                sm_op(lambda e: e.tensor_copy(out=sm[:, 5:6], in_=sm[:, 4:5]))
                sm_op(lambda e: e.tensor_scalar(out=sm[:, 17:18], in0=sm[:, 4:5], scalar1=float(W), scalar2=None, op0=ALU.mult))
                with cx.scope() as es4:
                    emit_sincos(cx, es4, sm[:, 5:6], t_sm, sm[:, 6:7], sm[:, 8:9], t_sm, [128, 1])
                    emit_sincos(cx, es4, sm[:, 17:18], t_sm, sm[:, 18:19], sm[:, 20:21], t_sm, [128, 1])
                sm_op(lambda e: e.tensor_tensor(out=sm[:, 9:10], in0=sm[:, 8:9], in1=sm[:, 3:4], op=ALU.mult))
                sm_op(lambda e: e.tensor_tensor(out=sm[:, 10:11], in0=sm[:, 6:7], in1=sm[:, 3:4], op=ALU.mult))
                sm_op(lambda e: e.tensor_tensor(out=sm[:, 15:17], in0=prm[:, 0:2], in1=prm[:, 0:2], op=ALU.mult), reads=[t_prm])
                sm_op(lambda e: e.tensor_tensor(out=sm[:, 11:12], in0=sm[:, 15:16], in1=sm[:, 16:17], op=ALU.add))
                sm_op(lambda e: e.reciprocal(out=sm[:, 11:12], in_=sm[:, 11:12]))
                sm_op(lambda e: e.tensor_scalar(out=sm[:, 12:13], in0=sm[:, 9:10], scalar1=-1.0, scalar2=None, op0=ALU.add))
                sm_op(lambda e: e.tensor_tensor(out=sm[:, 15:16], in0=sm[:, 12:13], in1=prm[:, 0:1], op=ALU.mult), reads=[t_prm])
                sm_op(lambda e: e.scalar_tensor_tensor(out=sm[:, 13:14], in0=sm[:, 10:11], scalar=prm[:, 1:2], in1=sm[:, 15:16], op0=ALU.mult, op1=ALU.add), reads=[t_prm])
                sm_op(lambda e: e.tensor_tensor(out=sm[:, 13:14], in0=sm[:, 13:14], in1=sm[:, 11:12], op=ALU.mult))
                sm_op(lambda e: e.tensor_tensor(out=sm[:, 15:16], in0=sm[:, 12:13], in1=prm[:, 1:2], op=ALU.mult), reads=[t_prm])
                sm_op(lambda e: e.scalar_tensor_tensor(out=sm[:, 14:15], in0=sm[:, 10:11], scalar=prm[:, 0:1], in1=sm[:, 15:16], op0=ALU.mult, op1=ALU.subtract), reads=[t_prm])
                sm_op(lambda e: e.tensor_tensor(out=sm[:, 14:15], in0=sm[:, 14:15], in1=sm[:, 11:12], op=ALU.mult))
                cx.op("dve", lambda e: e.tensor_scalar(out=bb[:, 0, :], in0=bt[:, 1, :], scalar1=sm[:, 14:15], scalar2=None, op0=ALU.mult),
                      reads=[t_bt, t_sm], writes=[t_bb])
                cx.op("dve", lambda e: e.scalar_tensor_tensor(out=bb[:, 0, :], in0=bt[:, 0, :], scalar=sm[:, 13:14], in1=bb[:, 0, :],
                                                              op0=ALU.mult, op1=ALU.subtract), reads=[t_bt, t_sm, t_bb], writes=[t_bb])
                cx.op("dve", lambda e: e.tensor_scalar(out=bb[:, 1, :], in0=bt[:, 0, :], scalar1=sm[:, 14:15], scalar2=None, op0=ALU.mult),
                      reads=[t_bt, t_sm], writes=[t_bb])
                cx.op("dve", lambda e: e.scalar_tensor_tensor(out=bb[:, 1, :], in0=bt[:, 1, :], scalar=sm[:, 13:14], in1=bb[:, 1, :],
                                                              op0=ALU.mult, op1=ALU.add), reads=[t_bt, t_sm, t_bb], writes=[t_bb])
                cx.op("dve", lambda e: e.memset(BP[:], 0.0), writes=[t_BP])
                cx.op("dve", lambda e: e.memset(CP[:], 0.0), writes=[t_CP])
                for g2 in range(2):
                    rs_ = slice(64 * g2, 64 * g2 + 64)
                    c_ = slice((go + g2) * 16, (go + g2) * 16 + 16)
                    for ri in range(2):
                        cx.op("dve", lambda e: e.tensor_copy(out=BP[rs_, ri, c_], in_=bb[rs_, ri, :]), reads=[t_bb], writes=[t_BP])
                    cx.op("dve", lambda e: e.tensor_copy(out=CP[rs_, 0, c_], in_=ctl[rs_, 0, :]), reads=[t_ctl], writes=[t_CP])
                    cx.op("dve", lambda e: e.tensor_scalar(out=CP[rs_, 1, c_], in0=ctl[rs_, 1, :], scalar1=-1.0, scalar2=None, op0=ALU.mult),
                          reads=[t_ctl], writes=[t_CP])
                for ri in range(2):
                    cx.op("pe", lambda e: e.transpose(out=sh.bank[4 + ri][:, 0:128], in_=BP[:, ri, :], identity=identf[:]),
                          reads=[t_BP, t_identf], writes=[sh.t_bank[4 + ri]])
                    cx.op("act", lambda e: e.copy(out=LB[:, ri, :], in_=sh.bank[4 + ri][:, 0:128]),
                          reads=[sh.t_bank[4 + ri]], writes=[t_LB])
                cx.op("dve", lambda e: e.tensor_scalar(out=T1[:], in0=IOT[:, d, :], scalar1=sm[:, 4:5], scalar2=None, op0=ALU.mult),
                      reads=[t_IOT, t_sm], writes=[t_T1])
                with cx.scope() as es4:
                    emit_sincos(cx, es4, T1[:], t_T1, SN[:], CN[:], t_tab, [128, W])
                cx.op("dve", lambda e: e.memset(carry[:], 0.0), writes=[t_carry])
                segs = range(nseg) if d == 0 else range(nseg - 1, -1, -1)
                for sg in segs:
                    c0 = sg * W
                    for q in range(W // 512):
                        cs = slice(q * 512, (q + 1) * 512)
                        gs = slice(c0 + q * 512, c0 + (q + 1) * 512)
                        br_, bi_ = (q % 2) * 2, (q % 2) * 2 + 1
                        for ri, b in ((0, br_), (1, bi_)):
                            cx.op("pe", lambda e: e.matmul(sh.bank[b][:], lhsT=LB[:, ri, :], rhs=ub[:, ct, gs], start=True, stop=True),
                                  reads=[t_LB, t_ub], writes=[sh.t_bank[b]])
                        cx.op("dve", lambda e: e.tensor_tensor(out=CR[:, cs], in0=sh.bank[br_][:], in1=CN[:, cs], op=ALU.mult),
                              reads=[sh.t_bank[br_], t_tab], writes=[t_CR])
                        cx.op("dve", lambda e: e.tensor_tensor(out=tmpa[:], in0=sh.bank[bi_][:], in1=SN[:, cs], op=ALU.mult),
                              reads=[sh.t_bank[bi_], t_tab], writes=[t_ta])
                        cx.op("dve", lambda e: e.tensor_tensor(out=CR[:, cs], in0=CR[:, cs], in1=tmpa[:], op=ALU.add),
                              reads=[t_CR, t_ta], writes=[t_CR])
                        cx.op("dve", lambda e: e.tensor_tensor(out=CI[:, cs], in0=sh.bank[bi_][:], in1=CN[:, cs], op=ALU.mult),
                              reads=[sh.t_bank[bi_], t_tab], writes=[t_CI])
                        cx.op("dve", lambda e: e.tensor_tensor(out=tmpb[:], in0=sh.bank[br_][:], in1=SN[:, cs], op=ALU.mult),
                              reads=[sh.t_bank[br_], t_tab], writes=[t_tb])
                        cx.op("dve", lambda e: e.tensor_tensor(out=CI[:, cs], in0=CI[:, cs], in1=tmpb[:], op=ALU.subtract),
                              reads=[t_CI, t_tb], writes=[t_CI])
                    rho = sm[:, 3:4].to_broadcast([128, W])
                    if d == 0:
                        cx.op("dve", lambda e: e.tensor_tensor_scan(out=VR[:], data0=rho, data1=CR[:], initial=carry[:, 0:1],
                                                                    op0=ALU.mult, op1=ALU.add), reads=[t_sm, t_CR, t_carry], writes=[t_VR])
                        cx.op("dve", lambda e: e.tensor_tensor_scan(out=VI[:], data0=rho, data1=CI[:], initial=carry[:, 1:2],
                                                                    op0=ALU.mult, op1=ALU.add), reads=[t_sm, t_CI, t_carry], writes=[t_VI])
                        er, ei = VR[:, W - 1:W], VI[:, W - 1:W]
                    else:
                        cx.op("dve", lambda e: e.tensor_tensor_scan(out=VR[:, ::-1], data0=rho, data1=CR[:, ::-1], initial=carry[:, 0:1],
                                                                    op0=ALU.mult, op1=ALU.add), reads=[t_sm, t_CR, t_carry], writes=[t_VR])
                        cx.op("dve", lambda e: e.tensor_tensor_scan(out=VI[:, ::-1], data0=rho, data1=CI[:, ::-1], initial=carry[:, 1:2],
                                                                    op0=ALU.mult, op1=ALU.add), reads=[t_sm, t_CI, t_carry], writes=[t_VI])
                        er, ei = VR[:, 0:1], VI[:, 0:1]
                    cx.op("dve", lambda e: e.tensor_tensor(out=carry[:, 2:3], in0=ei, in1=sm[:, 18:19], op=ALU.mult),
                          reads=[t_VI, t_sm], writes=[t_carry])
                    cx.op("dve", lambda e: e.tensor_tensor(out=carry[:, 3:4], in0=er, in1=sm[:, 18:19], op=ALU.mult),
                          reads=[t_VR, t_sm], writes=[t_carry])
                    cx.op("dve", lambda e: e.scalar_tensor_tensor(out=carry[:, 0:1], in0=er, scalar=sm[:, 20:21], in1=carry[:, 2:3],
                                                                  op0=ALU.mult, op1=ALU.subtract), reads=[t_VR, t_sm, t_carry], writes=[t_carry])
                    cx.op("dve", lambda e: e.scalar_tensor_tensor(out=carry[:, 1:2], in0=ei, scalar=sm[:, 20:21], in1=carry[:, 3:4],
                                                                  op0=ALU.mult, op1=ALU.add), reads=[t_VI, t_sm, t_carry], writes=[t_carry])
                    cx.op("pool", lambda e: e.tensor_tensor(out=T1[:], in0=CN[:], in1=VR[:], op=ALU.mult), reads=[t_tab, t_VR], writes=[t_T1])
                    cx.op("pool", lambda e: e.tensor_tensor(out=T2[:], in0=SN[:], in1=VI[:], op=ALU.mult), reads=[t_tab, t_VI], writes=[t_T2])
                    cx.op("pool", lambda e: e.tensor_tensor(out=XRb[:], in0=T1[:], in1=T2[:], op=ALU.subtract), reads=[t_T1, t_T2], writes=[t_XR])
                    cx.op("pool", lambda e: e.tensor_tensor(out=T1[:], in0=SN[:], in1=VR[:], op=ALU.mult), reads=[t_tab, t_VR], writes=[t_T1])
                    cx.op("pool", lambda e: e.tensor_tensor(out=T2[:], in0=CN[:], in1=VI[:], op=ALU.mult), reads=[t_tab, t_VI], writes=[t_T2])
                    cx.op("pool", lambda e: e.tensor_tensor(out=XIb[:], in0=T1[:], in1=T2[:], op=ALU.add), reads=[t_T1, t_T2], writes=[t_XI])
                    for q in range(W // 512):
                        cs = slice(q * 512, (q + 1) * 512)
                        gs = slice(c0 + q * 512, c0 + (q + 1) * 512)
                        b = 6 + (q % 2)
                        cx.op("pe", lambda e: e.matmul(sh.bank[b][:], lhsT=CP[:, 0, :], rhs=XRb[:, cs], start=True, stop=False),
                              reads=[t_CP, t_XR], writes=[sh.t_bank[b]], inc=False)
                        cx.op("pe", lambda e: e.matmul(sh.bank[b][:], lhsT=CP[:, 1, :], rhs=XIb[:, cs], start=False, stop=True),
                              reads=[t_CP, t_XI], writes=[sh.t_bank[b]])
                        cx.op("dve", lambda e: e.tensor_tensor(out=Y[:, ct, gs], in0=Y[:, ct, gs], in1=sh.bank[b][:], op=ALU.add),
                              reads=[t_Y, sh.t_bank[b]], writes=[t_Y])
        with cx.scope() as es2:
            yb = cx.sb(es2, [128, 2048], BF16, "s5yb")
            t_yb = cx.tok("s5yb")
            for ct in range(2):
                for q in range(SEQ // 2048):
                    gs = slice(q * 2048, (q + 1) * 2048)
                    with cx.scope() as es3:
                        emit_gelu_mul(cx, es3, Y[:, ct, gs], t_Y, None, None, yb[:], t_yb, [128, 2048])
                    cx.dma("sp", yT_rows[ct * 128:(ct + 1) * 128, gs], yb[:], reads=[t_yb], dram_write=True)


def const_inputs():
    t = np.arange(128)
    fwd = (t[None, :] >= t[:, None]).astype(np.float32)
    bwd = (t[None, :] <= t[:, None]).astype(np.float32)
    return {"cmask": np.ascontiguousarray(np.concatenate([fwd, bwd], axis=1)),
            "identf": np.eye(128, dtype=np.float32)}


def ab_mprm(P, hq):
    m = np.zeros((128, 8), np.float32)
    gb = P["mlstm_gate_b"][0]
    m[:, 0] = gb[0, 0, hq]; m[:, 1] = gb[0, 1, hq]; m[:, 2] = gb[1, 0, hq]; m[:, 3] = gb[1, 1, hq]
    nrm = P["mlstm_norm"][0][256 * hq:256 * hq + 256]
    m[:, 4] = nrm[0:128]; m[:, 5] = nrm[128:256]
    return m


def cd_host_tiles(P, hq):
    W = P["cd_w_in"][0]
    wlow = np.zeros((D, 128), np.float32)
    wlow[:, 0:16] = W[:, 3072:3088]
    wlow[:, 32:48] = W[:, 3088:3104]
    starts = [128 * hq, 512 + 128 * hq, 2048 + 256 * hq, 2048 + 256 * hq + 128]
    tiles = [W[:, s:s + 128] for s in starts] + [wlow, W[:, 3104 + 256 * hq:3104 + 256 * hq + 128],
                                                 W[:, 3104 + 256 * hq + 128:3104 + 256 * hq + 256]]
    out = {"cd_w": np.ascontiguousarray(np.stack([t.reshape(NDC, 128, 128).transpose(1, 0, 2) for t in tiles])),
           "cd_wv": tile_wide(W, 1024 + 256 * hq, 256)}
    gp = np.zeros((128, 8), np.float32)
    gp[:, 0] = P["gla_gate_b"][0][0, 128 * hq:128 * hq + 128]
    gp[:, 1] = P["gla_gate_b"][0][1, 128 * hq:128 * hq + 128]
    gp[:, 2] = P["gla_norm"][0][256 * hq:256 * hq + 128]
    gp[:, 3] = P["gla_norm"][0][256 * hq + 128:256 * hq + 256]
    out["gprm"] = gp
    wg2 = np.zeros((48, 128), np.float32)
    wg2[0:16] = P["gla_w_gate2"][0][0][:, 128 * hq:128 * hq + 128]
    wg2[32:48] = P["gla_w_gate2"][0][1][:, 128 * hq:128 * hq + 128]
    out["wg2"] = wg2
    s5p = np.zeros((2, 8, 128, 4), np.float32)
    s5b = np.zeros((8, 128, 2, 16), np.float32)
    s5c = np.zeros((8, 128, 2, 16), np.float32)
    for st in range(8):
        for g2 in range(2):
            g = 16 * hq + 2 * st + g2
            rs = slice(64 * g2, 64 * g2 + 64)
            for d in range(2):
                s5p[d, st, rs, 0] = P["s5_a_re"][0][d, g]
                s5p[d, st, rs, 1] = P["s5_a_im"][0][d, g]
                s5p[d, st, rs, 2] = P["s5_log_dt"][0][d, g]
            s5b[st, rs, 0] = P["s5_b_re"][0][g]
            s5b[st, rs, 1] = P["s5_b_im"][0][g]
            s5c[st, rs, 0] = P["s5_c_re"][0][g].T
            s5c[st, rs, 1] = P["s5_c_im"][0][g].T
    out["s5p"], out["s5b"], out["s5c"] = s5p, s5b, s5c
    out["s5d"] = np.ascontiguousarray(P["s5_d"][0][256 * hq:256 * hq + 256].reshape(2, 128, 1))
    return out


def norm_phase(cx, sh, xT, g_dram, outT, out_dt, ntok=TOK):
    TB = 1024 if out_dt == BF16 else 512
    with cx.scope() as es:
        g_sb, t_g = load_gain(cx, es, g_dram)
        hT = [cx.sb(es, [128, NDC, TB], out_dt, "nhT") for _ in range(2)]
        t_h = cx.toks(2, "nhT")
        ov = outT.rearrange("(dc p) t -> p dc t", p=128)
        for blk in range(ntok // TB):
            with cx.scope() as es2:
                emit_norm(cx, sh, es2, xT, blk * TB, TB, g_sb, t_g, hT[blk % 2], t_h[blk % 2], 0)
            cx.dma("sp", ov[:, :, blk * TB:(blk + 1) * TB], hT[blk % 2][:], reads=[t_h[blk % 2]], dram_write=True)
    cx.phase_end()


def outproj_phase(cx, sh, xT, xT_out, y_all, wout_dram, glu_w=None, ntok=TOK, TB=1024):
    with cx.scope() as es:
        yT = cx.sb(es, [128, 16, TB], BF16, "opy")
        t_y = cx.tok("opy")
        yv = y_all.rearrange("r (l p) t -> p (r l) t", p=128)
        if glu_w is not None:
            yd = cx.sb(es, [128, 8, TB], BF16, "opyd")
            t_yd = cx.tok("opyd")
            gw = [cx.sb(es, [128, 8, 128], BF16, "opgw") for _ in range(2)]
            t_gw = cx.toks(2, "opgw")
            sg = [cx.sb(es, [128, 512], F32, "opsg") for _ in range(2)]
            t_sg = cx.toks(2, "opsg")
        for blk in range(ntok // TB):
            tok0 = blk * TB
            cx.dma("sp", yT[:], yv[:, :, tok0:tok0 + TB], writes=[t_y])
            if glu_w is not None:
                k = 0
                for jc in range(8):
                    cx.dma("pool", gw[jc % 2][:], glu_w[jc], writes=[t_gw[jc % 2]])
                    fj = 4 * (jc // 2) + 2 + (jc % 2)
                    for tg in range(TB // 512):
                        b = k % 4
                        for kc in range(8):
                            fk = 4 * (kc // 2) + 2 + (kc % 2)
                            cx.op("pe", lambda e: e.matmul(sh.bank[b][:], lhsT=gw[jc % 2][:, kc, :],
                                                           rhs=yT[:, fk, tg * 512:(tg + 1) * 512],
                                                           start=(kc == 0), stop=(kc == 7)),
                                  reads=[t_gw[jc % 2], t_y], writes=[sh.t_bank[b]], inc=(kc == 7))
                        cx.op("act", lambda e: e.activation(out=sg[k % 2][:], in_=sh.bank[b][:], func=AF.Sigmoid),
                              reads=[sh.t_bank[b]], writes=[t_sg[k % 2]])
                        cx.op("dve", lambda e: e.tensor_tensor(out=yd[:, jc, tg * 512:(tg + 1) * 512], in0=sg[k % 2][:],
                                                               in1=yT[:, fj, tg * 512:(tg + 1) * 512], op=ALU.mult),
                              reads=[t_sg[k % 2], t_y], writes=[t_yd])
                        k += 1

                def act_fn(fc, c0, c1):
                    if fc % 4 >= 2:
                        return yd[:, 2 * (fc // 4) + (fc % 4 - 2), c0:c1], t_yd
                    return yT[:, fc, c0:c1], t_y
            else:
                def act_fn(fc, c0, c1):
                    return yT[:, fc, c0:c1], t_y
            with cx.scope() as es2:
                emit_down(cx, sh, es2, None, None, 16, TB, wout_dram, 1.0, xT, xT_out, tok0,
                          banks=(4, 5, 6, 7), act_fn=act_fn)
    cx.phase_end()


def mixer_ab_core(cx, sh, hn_all, T, projT, vtok, yT):
    proj_phase(cx, sh, hn_all, T["ab_w"], 10, projT, wv_dram=T["ab_wv"], vtok=vtok, VW=257, rep_dram=T["ab_rep"], nrep=4)
    for i in range(2):
        lru_tile(cx, sh, projT[i], projT[2 + i], T["lru_prm"][i], T["lru_gw"][i], yT[128 * i:128 * (i + 1), :])
        cx.phase_end()
    with cx.scope() as esc:
        cc = CLAConsts(cx, esc, T["cmask"], T["identf"])
        mlstm_phase(cx, sh, cc, projT, vtok, T["mprm"], yT[256:512, :])
    cx.phase_end()


def mixer_cd_core(cx, sh, hn_all, T, projT, vtok, yT):
    proj_phase(cx, sh, hn_all, T["cd_w"], 7, projT, wv_dram=T["cd_wv"], vtok=vtok, VW=256)
    with cx.scope() as esc:
        cc = CLAConsts(cx, esc, T["cmask"], T["identf"])
        gla_phase(cx, sh, cc, projT, vtok, T["gprm"], T["wg2"], yT[0:256, :])
    cx.phase_end()
    with cx.scope() as esc:
        idf = load_const(cx, esc, T["identf"], [128, 128], name="identf2")
        s5_phase(cx, sh, idf, [projT[5], projT[6]], T["s5p"], T["s5b"], T["s5c"], T["s5d"], yT[256:512, :])
    cx.phase_end()


def tile_gain(g):
    return np.ascontiguousarray(g.reshape(NDC, 128).T)


def wout_order():
    rows = []
    for r in range(4):
        for lc in range(4):
            base = 256 * r + 128 * lc if lc < 2 else 1024 + 256 * r + 128 * (lc - 2)
            rows.extend(range(base, base + 128))
    return np.array(rows)


class Prog:
    def __init__(self):
        self.nc = bass.Bass("TRN2", target_bir_lowering=False)
        self.in_names = []
        self.out_names = []

    def inp(self, name, shape, dt=F32):
        self.in_names.append(name)
        return self.nc.dram_tensor(name, list(shape), dt, kind="ExternalInput").ap()

    def out(self, name, shape, dt=F32):
        self.out_names.append(name)
        return self.nc.dram_tensor(name, list(shape), dt, kind="ExternalOutput").ap()

    def scratch(self, name, shape, dt=F32):
        return self.nc.dram_tensor(name, list(shape), dt, kind="Internal").ap()

    def run(self, in_maps):
        maps = [{k: m[k] for k in self.in_names} for m in in_maps]
        res = run_bass_kernel_spmd(self.nc, maps, core_ids=list(range(NCORES)))
        return res.results


def _decl_inputs(pg, d, skip=()):
    out = {}
    for k, v in d.items():
        if k in skip:
            continue
        out[k] = pg.inp(k, v.shape, BF16 if v.dtype != np.float32 else F32)
    return out


def kernel(**inputs):
    P = {k: np.asarray(v) for k, v in inputs.items()}
    x = P["x"]
    NB = 2
    cst = const_inputs()
    order = wout_order()
    shared = dict(cst)
    for l in range(2):
        for nm in ("ffn1", "ffn2"):
            shared["wgu_%s_%d" % (nm, l)] = tile_wgu(P[nm + "_w_gu"][l])
            shared["wdn_%s_%d" % (nm, l)] = tile_wdown(P[nm + "_w_down"][l], NFC)
            shared["g_%s_%d" % (nm, l)] = tile_gain(P["norm_" + nm][l])
        shared["gm_%d" % l] = tile_gain(P["norm_mix"][l])
    shared["gf"] = tile_gain(P["final_norm"])
    shared["wo_ab"] = tile_wdown(P["ab_w_out"][0][order], 16)
    shared["wo_cd"] = tile_wdown(P["cd_w_out"][0][order], 16)
    shared["wglu"] = tile_wdown(P["s5_w_glu"][0], 8, 8)
    for hq in range(4):
        for k, v in dict(ab_host_tiles(P, hq), mprm=ab_mprm(P, hq)).items():
            shared["%s_h%d" % (k, hq)] = v
        for k, v in cd_host_tiles(P, hq).items():
            shared["%s_h%d" % (k, hq)] = v

    pg = Prog()
    xin = pg.inp("xT", [D, SEQ])
    A = _decl_inputs(pg, shared)
    outT = pg.out("outT", [D, SEQ])
    xs = pg.scratch("xs", [D, SEQ])
    hn = pg.scratch("hn", [D, SEQ], BF16)
    yT = pg.scratch("yTs", [2048, SEQ], BF16)
    projT = pg.scratch("projT", [14, 128, SEQ])
    vtok = pg.scratch("vtok", [SEQ, 257], BF16)
    vtok2 = pg.scratch("vtok2", [SEQ, 256], BF16)
    y4 = yT.rearrange("(r f) t -> r f t", r=4)
    with ExitStack() as es:
        cx = Cx(pg.nc, es)
        sh = Shared(cx, es)

        def ffn(src, nm, l):
            ffn_phase(cx, sh, src, xs, A["g_%s_%d" % (nm, l)], A["wgu_%s_%d" % (nm, l)], A["wdn_%s_%d" % (nm, l)], ntok=SEQ)

        ffn(xin, "ffn1", 0)
        norm_phase(cx, sh, xs, A["gm_0"], hn, BF16, ntok=SEQ)
        for hq in range(4):
            T = {k: A["%s_h%d" % (k, hq)] for k in ("ab_w", "ab_wv", "ab_rep", "lru_prm", "lru_gw", "mprm")}
            T["cmask"], T["identf"] = A["cmask"], A["identf"]
            mixer_ab_core(cx, sh, hn, T, projT, vtok, yT[512 * hq:512 * (hq + 1), :])
        outproj_phase(cx, sh, xs, xs, y4, A["wo_ab"], ntok=SEQ)
        ffn(xs, "ffn2", 0)
        ffn(xs, "ffn1", 1)
        norm_phase(cx, sh, xs, A["gm_1"], hn, BF16, ntok=SEQ)
        for hq in range(4):
            T = {k: A["%s_h%d" % (k, hq)] for k in ("cd_w", "cd_wv", "gprm", "wg2", "s5p", "s5b", "s5c", "s5d")}
            T["cmask"], T["identf"] = A["cmask"], A["identf"]
            mixer_cd_core(cx, sh, hn, T, projT[0:7], vtok2, yT[512 * hq:512 * (hq + 1), :])
        outproj_phase(cx, sh, xs, xs, y4, A["wo_cd"], glu_w=A["wglu"], ntok=SEQ)
        ffn(xs, "ffn2", 1)
        norm_phase(cx, sh, xs, A["gf"], outT, F32, ntok=SEQ)
        cx.finish()
    maps = [dict(shared, xT=np.ascontiguousarray(x[b].T)) for b in range(NB)]
    res = run_bass_kernel_spmd(pg.nc, [{k: m[k] for k in pg.in_names} for m in maps], core_ids=list(range(NB)))
    out = np.empty((2, SEQ, D), np.float32)
    for b in range(NB):
        out[b] = res.results[b]["outT"].T
    return out
```
